# Optimizing a Trainium2 kernel written in Bass

```python
import math
import jax, jax.numpy as jnp
from jax import lax
import numpy as np

D_MODEL = 2048
BATCH = 16
SEQ = 2048
DEPTH = 4

N_META = 16
BLOCK_Q = 128
N_A_LAYERS = DEPTH // 2
N_B_LAYERS = DEPTH - N_A_LAYERS
DA_HEAD_DIM = 64
DA_N_HEADS = D_MODEL // (2 * DA_HEAD_DIM)
SB_HEAD_DIM = 128
SB_N_HEADS = D_MODEL // SB_HEAD_DIM
D_FF = 4 * D_MODEL
RMS_EPS = 1e-6
LAMBDA_STD = 0.1

kernel_name = "yoco_diffattn_stickbreaking_hybrid"


def rmsnorm(x, g):
    xf = x.astype(jnp.float32)
    y = xf * lax.rsqrt(jnp.mean(xf * xf, axis=-1, keepdims=True) + RMS_EPS)
    return (y * g.astype(jnp.float32)).astype(x.dtype)


def alibi_slopes(n_heads):
    return jnp.asarray(np.array([2.0 ** (-8.0 * (i + 1) / n_heads) for i in range(n_heads)], dtype=np.float32))


def query_blocks(total_len):
    bounds = [(0, N_META)]
    for s in range(N_META, total_len, BLOCK_Q):
        bounds.append((s, min(s + BLOCK_Q, total_len)))
    return bounds


def diff_attention(h, w_qkv, w_o, lam_q1, lam_k1, lam_q2, lam_k2, subln_g, lambda_init):
    b, L, _ = h.shape
    q, k, v = jnp.split(h @ w_qkv, 3, axis=-1)
    q = q.reshape(b, L, DA_N_HEADS, 2, DA_HEAD_DIM)
    k = k.reshape(b, L, DA_N_HEADS, 2, DA_HEAD_DIM)
    v = v.reshape(b, L, DA_N_HEADS, 2 * DA_HEAD_DIM)
    f32 = jnp.float32
    lam = (jnp.exp(jnp.sum(lam_q1.astype(f32) * lam_k1.astype(f32)))
           - jnp.exp(jnp.sum(lam_q2.astype(f32) * lam_k2.astype(f32))) + lambda_init)
    slopes = alibi_slopes(DA_N_HEADS)
    scale = DA_HEAD_DIM ** -0.5
    pos = jnp.arange(L)
    outs = []
    for s0, s1 in query_blocks(L):
        qb = q[:, s0:s1].astype(f32) * scale
        kb = k[:, :s1].astype(f32)
        scores = jnp.einsum('bqhcd,bkhcd->bhcqk', qb, kb)
        dist = (pos[s0:s1, None] - pos[None, :s1]).astype(f32)
        bias = jnp.where(dist[None] >= 0, -slopes[:, None, None] * dist[None], -jnp.inf)
        p = jax.nn.softmax(scores + bias[None, :, None], axis=-1)
        attn = p[:, :, 0] - lam * p[:, :, 1]
        outs.append(jnp.einsum('bhqk,bkhe->bqhe', attn, v[:, :s1].astype(f32)))
    o = jnp.concatenate(outs, axis=1)
    o = rmsnorm(o, subln_g) * (1.0 - lambda_init)
    return o.reshape(b, L, D_MODEL).astype(h.dtype) @ w_o


def stick_breaking_attention(h, w_q, k, v, w_o):
    b, L, _ = h.shape
    f32 = jnp.float32
    q = (h @ w_q).reshape(b, L, SB_N_HEADS, SB_HEAD_DIM)
    scale = SB_HEAD_DIM ** -0.5
    pos = jnp.arange(L)
    outs = []
    for s0, s1 in query_blocks(L):
        qb = q[:, s0:s1].astype(f32) * scale
        z = jnp.einsum('bqhd,bkhd->bhqk', qb, k[:, :s1].astype(f32))
        strict = pos[s0:s1, None] > pos[None, :s1]
        log_beta = jax.nn.log_sigmoid(z)
        log_1m_beta = jnp.where(strict, jax.nn.log_sigmoid(-z), 0.0)
        log_keep = lax.cumsum(log_1m_beta, axis=3, reverse=True) - log_1m_beta
        attn = jnp.where(strict, jnp.exp(log_beta + log_keep), 0.0)
        outs.append(jnp.einsum('bhqk,bkhd->bqhd', attn, v[:, :s1].astype(f32)))
    o = jnp.concatenate(outs, axis=1)
    return o.reshape(b, L, D_MODEL).astype(h.dtype) @ w_o


def sq_relu_mlp(h, w_up, w_down):
    a = jax.nn.relu(h @ w_up)
    return (a * a) @ w_down


def setup_inputs(seed: int = 0) -> dict:
    key = jax.random.key(seed)
    ks = jax.random.split(key, 20)
    D, F = D_MODEL, D_FF
    nrm = jax.random.normal
    return {
        "x": nrm(ks[0], (BATCH, SEQ, D), jnp.float32),
        "meta_tokens": nrm(ks[1], (N_META, D), jnp.float32),
        "attn_norm_g": 1.0 + 0.02 * nrm(ks[2], (DEPTH, D), jnp.float32),
        "mlp_norm_g": 1.0 + 0.02 * nrm(ks[3], (DEPTH, D), jnp.float32),
        "da_w_qkv": nrm(ks[4], (N_A_LAYERS, D, 3 * D), jnp.float32) * D ** -0.5,
        "da_w_o": nrm(ks[5], (N_A_LAYERS, D, D), jnp.float32) * D ** -0.5,
        "da_lambda_q1": LAMBDA_STD * nrm(ks[6], (N_A_LAYERS, DA_HEAD_DIM), jnp.float32),
        "da_lambda_k1": LAMBDA_STD * nrm(ks[7], (N_A_LAYERS, DA_HEAD_DIM), jnp.float32),
        "da_lambda_q2": LAMBDA_STD * nrm(ks[8], (N_A_LAYERS, DA_HEAD_DIM), jnp.float32),
        "da_lambda_k2": LAMBDA_STD * nrm(ks[9], (N_A_LAYERS, DA_HEAD_DIM), jnp.float32),
        "da_subln_g": 1.0 + 0.02 * nrm(ks[10], (N_A_LAYERS, 2 * DA_HEAD_DIM), jnp.float32),
        "kv_norm_g": 1.0 + 0.02 * nrm(ks[11], (D,), jnp.float32),
        "sb_w_k": nrm(ks[12], (D, D), jnp.float32) * D ** -0.5,
        "sb_w_v": nrm(ks[13], (D, D), jnp.float32) * D ** -0.5,
        "sb_w_q": nrm(ks[14], (N_B_LAYERS, D, D), jnp.float32) * D ** -0.5,
        "sb_w_o": nrm(ks[15], (N_B_LAYERS, D, D), jnp.float32) * D ** -0.5,
        "mlp_w_up": nrm(ks[16], (DEPTH, D, F), jnp.float32) * D ** -0.5,
        "mlp_w_down": nrm(ks[17], (DEPTH, F, D), jnp.float32) * F ** -0.5,
        "final_norm_g": 1.0 + 0.02 * nrm(ks[18], (D,), jnp.float32),
    }


def reference(x, meta_tokens, attn_norm_g, mlp_norm_g, da_w_qkv, da_w_o,
              da_lambda_q1, da_lambda_k1, da_lambda_q2, da_lambda_k2, da_subln_g,
              kv_norm_g, sb_w_k, sb_w_v, sb_w_q, sb_w_o, mlp_w_up, mlp_w_down,
              final_norm_g):
    b = x.shape[0]
    meta = jnp.broadcast_to(meta_tokens.astype(x.dtype)[None], (b, N_META, D_MODEL))
    h = jnp.concatenate([meta, x], axis=1)
    L = h.shape[1]
    k_shared = v_shared = None
    for i in range(DEPTH):
        if i < N_A_LAYERS:
            lambda_init = 0.8 - 0.6 * math.exp(-0.3 * i)
            h = h + diff_attention(rmsnorm(h, attn_norm_g[i]), da_w_qkv[i], da_w_o[i],
                                   da_lambda_q1[i], da_lambda_k1[i], da_lambda_q2[i],
                                   da_lambda_k2[i], da_subln_g[i], lambda_init)
        else:
            if i == N_A_LAYERS:
                hk = rmsnorm(h, kv_norm_g)
                k_shared = (hk @ sb_w_k).reshape(b, L, SB_N_HEADS, SB_HEAD_DIM)
                v_shared = (hk @ sb_w_v).reshape(b, L, SB_N_HEADS, SB_HEAD_DIM)
            j = i - N_A_LAYERS
            h = h + stick_breaking_attention(rmsnorm(h, attn_norm_g[i]), sb_w_q[j],
                                             k_shared, v_shared, sb_w_o[j])
        h = h + sq_relu_mlp(rmsnorm(h, mlp_norm_g[i]), mlp_w_up[i], mlp_w_down[i])
    return rmsnorm(h, final_norm_g)[:, N_META:]
```

```python
import math
from contextlib import ExitStack

import numpy as np
import ml_dtypes

import concourse.bass as bass
import concourse.mybir as mybir
from concourse.bass_utils import run_bass_kernel_spmd

F32 = mybir.dt.float32
BF16 = mybir.dt.bfloat16
AF = mybir.ActivationFunctionType
ALU = mybir.AluOpType
AX = mybir.AxisListType

NMETA = 16
RMS_EPS = 1e-6
ENGINES = ("pe", "act", "dve", "pool", "sp")
SEM_RING = 8
DMA_RING = {"sp": 40, "pool": 16}


class Cfg:
    def __init__(self, D=2048, S=2048, NSEQ=2, NA=2, NB=2, force_w=None):
        self.D, self.S, self.NSEQ, self.NA, self.NB = D, S, NSEQ, NA, NB
        self.DEPTH = NA + NB
        self.NC = D // 128
        self.F = 4 * D
        self.T = S + NMETA
        self.NT = S // 128 + 1
        self.NQ = S // 512 + 1
        self.G = 2
        self.WCA = 128 * self.G
        self.WC = min(512, D)
        self.NG = self.NC // self.G
        self.force_w = force_w

    def tile_rng(self, t):
        return (0, NMETA) if t == 0 else (NMETA + 128 * (t - 1), 128)

    def chunk_rng(self, c):
        return (0, NMETA) if c == 0 else (NMETA + 512 * (c - 1), 512)

    def tiles_of_chunk(self, c):
        return [0] if c == 0 else list(range(4 * (c - 1) + 1, 4 * c + 1))

    def chunk_of_tile(self, t):
        return 0 if t == 0 else (t - 1) // 4 + 1


class _Op:
    __slots__ = ("eng", "fn", "deps", "dma", "signum", "dsem", "dval", "need", "tag", "nmm")

    def __init__(self, eng, fn, deps, dma):
        self.eng, self.fn, self.deps, self.dma = eng, fn, deps, dma
        self.signum = None
        self.dsem = None
        self.dval = None
        self.need = False
        self.tag = None
        self.nmm = 0


class _CntProxy:
    def __init__(self, e):
        self.e = e
        self.n = 0

    def matmul(self, *a, **k):
        self.n += 1
        return self.e.matmul(*a, **k)

    def transpose(self, *a, **k):
        self.n += 1
        return self.e.transpose(*a, **k)


class Rec:
    def __init__(self):
        self.ops = []
        self.lastw = {}
        self.rd = {}
        self.last_on = {}
        self.unobs = {"sp": set(), "pool": set()}
        self.bar_deps = set()
        self.bar_pending = set()

    max_ops = None
    tag = None

    def op(self, eng, fn, reads=(), writes=(), dma=False, nobar=False, force=False):
        if self.max_ops is not None and len(self.ops) >= self.max_ops and not force:
            return None
        idx = len(self.ops)
        psr = [k for k in reads if k[0] == "ps"]
        if psr:
            reads = [k for k in reads if k[0] != "ps"]
            writes = list(writes) + [k for k in psr if k not in writes]
        deps = set()
        for k in reads:
            w = self.lastw.get(k)
            if w is not None:
                deps.add(w)
        for k in writes:
            w = self.lastw.get(k)
            if w is not None:
                deps.add(w)
            r = self.rd.get(k)
            if r:
                for v in r.values():
                    if isinstance(v, list):
                        deps.update(v)
                    else:
                        deps.add(v)
        if not nobar and eng in self.bar_pending:
            deps |= self.bar_deps
            self.bar_pending.discard(eng)
        for k in reads:
            r = self.rd.setdefault(k, {})
            if dma:
                r.setdefault(("dma", eng), []).append(idx)
            else:
                r[eng] = idx
        for k in writes:
            self.lastw[k] = idx
            self.rd[k] = {}
        deps.discard(idx)
        for d in deps:
            od = self.ops[d]
            if od.dma:
                self.unobs[od.eng].discard(d)
        self.ops.append(_Op(eng, fn, deps, dma))
        self.ops[-1].tag = self.tag
        if not nobar:
            self.last_on[eng] = idx
        if dma and not nobar:
            self.unobs[eng].add(idx)
        return idx

    def barrier(self):
        for q in ("sp", "pool"):
            un = set(self.unobs[q])
            if un:
                idx = len(self.ops)
                self.ops.append(_Op(q, None, un, False))
                self.last_on[q] = idx
                self.unobs[q] = set()
        self.bar_deps = set(self.last_on.values())
        self.bar_pending = set(ENGINES)

    def emit(self, nc, block, sems, dsems):
        ops = self.ops
        for o in ops:
            for d in o.deps:
                ops[d].need = True
        cnt = {e: 0 for e in ENGINES}
        dcnt = {"sp": 0, "pool": 0}
        for o in ops:
            if o.dma:
                j = dcnt[o.eng]
                dcnt[o.eng] += 1
                K = len(dsems[o.eng])
                o.dsem = dsems[o.eng][j % K]
                o.dval = 16 * (j // K + 1)
            elif o.need:
                cnt[o.eng] += 1
                o.signum = cnt[o.eng]
        per_eng = {e: [] for e in ENGINES}
        for i, o in enumerate(ops):
            per_eng[o.eng].append(i)

        def run(ename, eng):
            waited = {e: 0 for e in ENGINES}
            dwaited = {}
            for i in per_eng[ename]:
                o = ops[i]
                need_c = {}
                for d in o.deps:
                    od = ops[d]
                    if od.dma:
                        key = id(od.dsem)
                        if dwaited.get(key, 0) < od.dval:
                            dwaited[key] = od.dval
                            eng.wait_ge(od.dsem, od.dval)
                    else:
                        if od.eng == "pe" and ename == "pe":
                            continue
                        if od.signum > need_c.get(od.eng, 0):
                            need_c[od.eng] = od.signum
                for e2, sn in need_c.items():
                    if sn > waited[e2]:
                        waited[e2] = sn
                        eng.wait_ge(sems[e2][(sn - 1) % SEM_RING], (sn - 1) // SEM_RING + 1)
                if ename == "pe" and o.fn is not None:
                    cp = _CntProxy(eng)
                    ins = o.fn(cp)
                    o.nmm = cp.n
                else:
                    ins = o.fn(eng) if o.fn is not None else None
                if o.dma:
                    ins.then_inc(o.dsem, 16)
                elif o.signum is not None:
                    s = sems[ename][(o.signum - 1) % SEM_RING]
                    if ins is None:
                        eng.sem_inc(s, 1)
                    else:
                        ins.then_inc(s, 1)

        @block.tensor
        def _(e):
            run("pe", e)

        @block.scalar
        def _(e):
            run("act", e)

        @block.vector
        def _(e):
            run("dve", e)

        @block.gpsimd
        def _(e):
            run("pool", e)

        @block.sync
        def _(e):
            run("sp", e)


def alibi_slope(h, nh):
    return 2.0 ** (-8.0 * (h + 1) / nh)


class Builder:
    def __init__(self, cfg, debug=False):
        self.cfg = cfg
        self.debug = debug
        self.rec = Rec()
        self.bt_cols = {}
        self.gemm_bank = 0
        self.wptr = 0
        self.hs_owner = [None] * 4
        self.da_cnt = 0
        self.deferred = None
        self.w_done = set()
        self.evac_flip = 0

    def declare(self, nc):
        c = self.cfg
        D, S, T, NC, F = c.D, c.S, c.T, c.NC, c.F
        dt = {}

        def inp(name, shape, dtype=F32):
            dt[name] = nc.dram_tensor(name, list(shape), dtype, kind="ExternalInput").ap()

        inp("x", [c.NSEQ, S, D])
        inp("meta", [NMETA, D])
        inp("attn_g", [c.DEPTH, D])
        inp("mlp_g", [c.DEPTH, D])
        inp("wqkv", [c.NA, D, 3 * D])
        inp("wo_a", [c.NA, D, D])
        inp("lq1", [c.NA, 64])
        inp("lk1", [c.NA, 64])
        inp("lq2", [c.NA, 64])
        inp("lk2", [c.NA, 64])
        inp("subln", [c.NA, 128])
        inp("kvg", [1, D])
        inp("wk", [D, D])
        inp("wv", [D, D])
        inp("wq_b", [c.NB, D, D])
        inp("wo_b", [c.NB, D, D])
        inp("wup", [c.DEPTH, D, F])
        inp("wdown", [c.DEPTH, F, D])
        inp("fng", [1, D])
        inp("cb16", [128, 8 * 128], BF16)
        self.NBT = NC * 96
        inp("bt", [128, self.NBT])
        dt["out"] = nc.dram_tensor("out", [c.NSEQ, S, D], F32, kind="ExternalOutput").ap()
        dt["h"] = nc.dram_tensor("h_scr", [T, D], F32).ap()
        dt["ons"] = nc.dram_tensor("ons_scr", [NC, 128, T], BF16).ap()
        dt["ks"] = nc.dram_tensor("ks_scr", [NC, 128, T], BF16).ap()
        dt["vs"] = nc.dram_tensor("vs_scr", [T, D], BF16).ap()
        self.dt = dt

    def allocate(self, nc, es):
        c = self.cfg
        D, T, NC, NT, G = c.D, c.T, c.NC, c.NT, c.G

        def sb(name, cols, dtype):
            return es.enter_context(nc.sbuf_tensor(name, [128, cols], dtype))

        self.A = sb("A", NC * T, BF16)
        self.oQT = 0
        self.oKT = G * T
        self.oV = 2 * G * T
        self.oSP = 2 * G * T + NT * c.WCA
        self.oSP += self.oSP % 2
        spare = max(8 * D, 8 * 1024 + 2048)
        self.Bcols = max(NC * T, self.oSP + spare)
        self.Bcols += self.Bcols % 2
        self.B = sb("B", self.Bcols, BF16)
        self.Wt = sb("Wt", 4 * NC * 256, BF16)
        self.RBALL = sb("RBALL", 8 * 512, F32)
        self.RB = [self.RBALL[:, i * 512:(i + 1) * 512] for i in range(8)]
        self.ON = [sb(f"ON{i}", T, BF16) for i in range(2)]
        self.P = [[sb(f"P{m}{b}", 512, BF16) for b in range(2)] for m in range(2)]
        self.CB = sb("CB", 8 * 128, BF16)
        self.BT = sb("BT", self.NBT, F32)
        self.SM = sb("SM", 4 * 64 + 64, F32)
        self.EPS = self.SM[:, 270:271]
        self.RT = [sb(f"RT{i}", 512, F32) for i in range(2)]
        self.ps = [es.enter_context(nc.psum_tensor(f"ps{i}", [128, 512], F32)) for i in range(8)]
        self.ident = self.CB[:, 0:128]
        self.ones = self.CB[:, 128:256]
        self.tri_le = self.CB[:, 256:384]
        self.tri_lt = self.CB[:, 384:512]
        self.tincl = self.CB[:, 512:640]
        self.zeros = self.CB[:, 640:768]
        self.ntincl = self.CB[:, 768:896]
        self.nones = self.CB[:, 896:1024]

    def Bf32(self, off_bf16, cols_f32):
        return self.B[:, off_bf16:off_bf16 + 2 * cols_f32].bitcast(F32)

    def op(self, *a, **k):
        return self.rec.op(*a, **k)

    def next_bank(self):
        b = self.gemm_bank % 4
        self.gemm_bank += 1
        return b

    def peek_w(self, big):
        p = self.wptr
        if big:
            if p % 2:
                p += 1
            return p % 4, 2, p + 2
        return p % 4, 1, p + 1

    def alloc_w(self, big):
        hs, nh, p = self.peek_w(big)
        self.wptr = p
        return hs, nh

    def load_w(self, src, ncols):
        c = self.cfg
        NC = c.NC
        big = ncols > 256
        hs, nh = self.alloc_w(big)
        base = hs * NC * 256
        view = self.Wt[:, base:base + NC * ncols].rearrange("p (k c) -> p k c", k=NC)
        nsplit = 4 if NC >= 4 else 1
        kper = NC // nsplit
        keys = []
        hkeys = [("W", hs + i) for i in range(nh)]
        for s in range(nsplit):
            key = ("Wp", hs, s)
            keys.append(key)
            srcv = src[s * kper * 128:(s + 1) * kper * 128, :].rearrange("(k p) c -> p k c", p=128)
            dstv = view[:, s * kper:(s + 1) * kper, :]

            def fn(e, dstv=dstv, srcv=srcv):
                return e.dma_start(out=dstv, in_=srcv)
            self.op("pool", fn, reads=(), writes=[key] + hkeys, dma=True, nobar=True)
        return view, keys + hkeys

    def evac_engine(self):
        self.evac_flip ^= 1
        return "act" if self.evac_flip else "dve"

    def copy_op(self, eng, out, in_, reads, writes):
        if eng == "act":
            def fn(e):
                return e.copy(out, in_)
        else:
            def fn(e):
                return e.tensor_copy(out, in_)
        self.op(eng, fn, reads, writes)

    def a_keys_chunk(self, cidx):
        c = self.cfg
        return [("A", k, t) for k in range(c.NC) for t in c.tiles_of_chunk(cidx)]

    def gemm_F(self, X, xkeys_fn, wview, wkeys, ncols, evac):
        c = self.cfg
        NC, T = c.NC, c.T
        for j in range(ncols // 128):
            for ci in range(c.NQ):
                cs, n = c.chunk_rng(ci)
                bank = self.next_bank()
                ps = self.ps[bank]
                pairs = [(wview[:, k, j * 128:(j + 1) * 128], X[:, k * T + cs:k * T + cs + n]) for k in range(NC)]
                out = ps[:, 0:n]

                def fn(e, pairs=pairs, out=out):
                    nn = len(pairs)
                    for i, (l, r) in enumerate(pairs):
                        ins = e.matmul(out, l, r, start=(i == 0), stop=(i == nn - 1))
                    return ins
                self.op("pe", fn, reads=list(wkeys) + xkeys_fn(ci), writes=[("ps", bank)])
                evac(j, ci, ps[:, 0:n], bank)

    def gemm_T(self, X, xkeys_fn, wview, wkeys, ncols, evac, pre=None):
        c = self.cfg
        NC, T = c.NC, c.T
        PF = 4
        if pre is not None:
            for t in range(min(PF, c.NT)):
                pre(t)
        for t in range(c.NT):
            ts, n = c.tile_rng(t)
            if pre is not None and t + PF < c.NT:
                pre(t + PF)
            bank = self.next_bank()
            ps = self.ps[bank]
            pairs = [(X[:, k * T + ts:k * T + ts + n], wview[:, k, :]) for k in range(NC)]
            out = ps[0:n, 0:ncols]

            def fn(e, pairs=pairs, out=out):
                nn = len(pairs)
                for i, (l, r) in enumerate(pairs):
                    ins = e.matmul(out, l, r, start=(i == 0), stop=(i == nn - 1))
                return ins
            self.op("pe", fn, reads=list(wkeys) + xkeys_fn(t), writes=[("ps", bank)])
            evac(t, out, bank)

    def make_rmw(self, col0, ncols):
        c = self.cfg
        h = self.dt["h"]
        state = {"i": 0, "slot": {}}

        def pre(t):
            ts, n = c.tile_rng(t)
            s = state["i"] % 8
            state["i"] += 1
            state["slot"][t] = s
            rb = self.RB[s]
            src = h[ts:ts + n, col0:col0 + ncols]
            dst = rb[0:n, 0:ncols]
            self.op("sp", lambda e: e.dma_start(out=dst, in_=src),
                    reads=[("h", t, col0 // 128 + q) for q in range(ncols // 128)], writes=[("RB", s)], dma=True)

        def evac(t, ps_ap, bank):
            ts, n = c.tile_rng(t)
            s = state["slot"][t]
            rb = self.RB[s][0:n, 0:ncols]
            self.op("dve", lambda e: e.tensor_tensor(rb, rb, ps_ap, ALU.add),
                    reads=[("ps", bank), ("RB", s)], writes=[("RB", s)])
            dst = h[ts:ts + n, col0:col0 + ncols]
            self.op("sp", lambda e: e.dma_start(out=dst, in_=rb),
                    reads=[("RB", s)], writes=[("h", t, col0 // 128 + q) for q in range(ncols // 128)], dma=True)
        return pre, evac

    def norm(self, g_ap, final_seq=None):
        c = self.cfg
        D, NC, T = c.D, c.NC, c.T
        rec = self.rec
        rec.barrier()
        o = self.oSP
        HT = [self.Bf32(o + 2 * D * i, D) for i in range(3)]
        HN = [self.B[:, o + 6 * D:o + 7 * D], self.B[:, o + 7 * D:o + 8 * D]]
        assert D <= 4096
        GT = self.RBALL[:, 0:D]
        SSQ = [self.SM[:, 256 + 2 * s:256 + 2 * s + 1] for s in range(2)]
        RSTD = [self.SM[:, 257 + 2 * s:257 + 2 * s + 1] for s in range(2)]
        EPS = self.EPS
        h = self.dt["h"]
        self.op("sp", lambda e: e.dma_start(out=GT, in_=g_ap.partition_broadcast(128)),
                reads=(), writes=[("GT",)], dma=True)
        A3 = self.A[:, :].rearrange("p (k t) -> p k t", k=NC)
        tiles = [t for t in range(c.NT) if not (final_seq is not None and t == 0)]

        def views(idx):
            t = tiles[idx]
            ts, n = c.tile_rng(t)
            hs, s = idx % 3, idx % 2
            return t, ts, n, hs, s, HT[hs][0:n, :], HN[s][0:n, :], SSQ[s][0:n, :], RSTD[s][0:n, :], GT[0:n, :]

        def st_load(idx):
            t, ts, n, hs, s, ht, hn, ssq, rstd, gt = views(idx)
            hsrc = h[ts:ts + n, :]
            self.op("sp", lambda e, ht=ht, hsrc=hsrc: e.dma_start(out=ht, in_=hsrc),
                    reads=[("h", t, q) for q in range(NC)], writes=[("HT", hs)], dma=True)

        def st_1a(idx):
            t, ts, n, hs, s, ht, hn, ssq, rstd, gt = views(idx)
            self.op("dve", lambda e, ssq=ssq: e.memset(ssq, 0.0), reads=(), writes=[("SSQ", s)])
            self.op("act", lambda e, hn=hn, ht=ht, ssq=ssq: e.activation(hn, ht, AF.Square, accum_out=ssq),
                    reads=[("HT", hs), ("SSQ", s)], writes=[("HN", s), ("SSQ", s)])
            self.op("act", lambda e, rstd=rstd, ssq=ssq, epsn=EPS[0:n, :]: e.activation(rstd, ssq, AF.Sqrt, bias=epsn, scale=1.0 / D),
                    reads=[("SSQ", s), ("EPS",)], writes=[("RSTD", s)])

        def st_1b(idx):
            t, ts, n, hs, s, ht, hn, ssq, rstd, gt = views(idx)
            self.op("dve", lambda e, rstd=rstd: e.reciprocal(rstd, rstd),
                    reads=[("RSTD", s)], writes=[("RSTD", s)])

        def st_2(idx):
            t, ts, n, hs, s, ht, hn, ssq, rstd, gt = views(idx)
            if final_seq is None:
                self.op("dve", lambda e, hn=hn, ht=ht, rstd=rstd, gt=gt:
                        e.scalar_tensor_tensor(hn, ht, rstd, gt, ALU.mult, ALU.mult),
                        reads=[("HT", hs), ("RSTD", s), ("GT",), ("HN", s)], writes=[("HN", s)])
                for k0 in range(0, NC, 8):
                    kk = min(8, NC - k0)
                    bank = 6 + (k0 // 8) % 2
                    pb = self.ps[bank][:, :].bitcast(BF16)
                    ident = self.ident[0:n, 0:n]

                    def fn(e, k0=k0, kk=kk, pb=pb, hn=hn, ident=ident, n=n):
                        for q in range(kk):
                            ins = e.transpose(pb[:, q * 128:q * 128 + n], hn[:, (k0 + q) * 128:(k0 + q + 1) * 128], ident)
                        return ins
                    self.op("pe", fn, reads=[("HN", s), ("CB",)], writes=[("ps", bank)])
                    src = pb[:, 0:kk * 128].rearrange("p (k t) -> p k t", k=kk)[:, :, 0:n]
                    dst = A3[:, k0:k0 + kk, ts:ts + n]
                    self.copy_op(self.evac_engine(), dst, src, reads=[("ps", bank)],
                                 writes=[("A", k, t) for k in range(k0, k0 + kk)])
            else:
                self.op("dve", lambda e, ht=ht, rstd=rstd, gt=gt:
                        e.scalar_tensor_tensor(ht, ht, rstd, gt, ALU.mult, ALU.mult),
                        reads=[("HT", hs), ("RSTD", s), ("GT",)], writes=[("HT", hs)])
                dst = self.dt["out"][final_seq, ts - NMETA:ts - NMETA + n, :]
                self.op("sp", lambda e, dst=dst, ht=ht: e.dma_start(out=dst, in_=ht),
                        reads=[("HT", hs)], writes=[("out", final_seq, t)], dma=True)

        nt = len(tiles)
        for i in range(min(2, nt)):
            st_load(i)
        st_1a(0)
        st_1b(0)
        for i in range(nt):
            if i + 2 < nt:
                st_load(i + 2)
            if i + 1 < nt:
                st_1a(i + 1)
            st_2(i)
            if i + 1 < nt:
                st_1b(i + 1)
        rec.barrier()

    def bt_col(self, h, delta):
        key = (h, delta)
        if key not in self.bt_cols:
            assert len(self.bt_cols) < self.NBT
            self.bt_cols[key] = len(self.bt_cols)
        return self.bt_cols[key]

    def act_width(self, h):
        c = self.cfg
        if c.force_w is not None:
            return c.force_w
        sl = alibi_slope(h, c.NC)
        for w in (512, 256, 128):
            if sl * w / 2 <= 50.0:
                return w
        raise AssertionError

    def flush_deferred(self):
        d = self.deferred
        if d is not None:
            self.deferred = None
            d()

    def da_head(self, layer_idx, hd, j, on_slot, NLAM, GS, final_cb=None):
        c = self.cfg
        T, WCA = c.T, c.WCA
        QT = self.B[:, self.oQT + j * T:self.oQT + (j + 1) * T]
        KT = self.B[:, self.oKT + j * T:self.oKT + (j + 1) * T]
        Vb = self.B[:, self.oV:self.oV + c.NT * WCA]
        ONb = self.ON[on_slot]
        o = self.oSP
        TMP = [self.Bf32(o + 1024 * i, 512) for i in range(4)]
        SQ = self.B[:, o + 4096:o + 4096 + 512]
        wact = self.act_width(hd)
        for ci in range(c.NQ):
            q0, N = c.chunk_rng(ci)
            tw = min(128, N)
            w = min(wact, N)
            ktiles = [0] if ci == 0 else [0] + list(range(1, 4 * ci + 1))
            diag0 = 0 if ci == 0 else 4 * (ci - 1) + 1
            nk = len(ktiles)

            def geom(t):
                kcol, kn = c.tile_rng(t)
                isdiag = (t >= diag0) if ci > 0 else True
                lo = tw * (t - diag0) if isdiag else 0
                return kcol, kn, isdiag, lo

            def stage_qk(ki):
                t = ktiles[ki]
                kcol, kn, isdiag, lo = geom(t)
                b = self.da_cnt % 2
                self.da_cnt += 1
                bsel[ki] = b
                for m in range(2):
                    bank = 2 * b + m
                    out = self.ps[bank][0:kn, lo:N]
                    lhsT = KT[64 * m:64 * m + 64, kcol:kcol + kn]
                    rhs = QT[64 * m:64 * m + 64, q0 + lo:q0 + N]
                    self.op("pe", lambda e, out=out, lhsT=lhsT, rhs=rhs: e.matmul(out, lhsT, rhs, start=True, stop=True),
                            reads=[("QT", j, ci), ("KT", j, c.chunk_of_tile(t))], writes=[("ps", bank)])
                for m in range(2):
                    bank = 2 * b + m
                    Pt = self.P[m][b]
                    for blk in range(N // w):
                        r0, r1 = max(lo, blk * w), (blk + 1) * w
                        if r1 <= r0:
                            continue
                        ref = q0 + blk * w + w // 2
                        col = self.bt_col(hd, kcol - ref)
                        bias = self.BT[0:kn, col:col + 1]
                        out = Pt[0:kn, r0:r1]
                        in_ = self.ps[bank][0:kn, r0:r1]
                        self.op("act", lambda e, out=out, in_=in_, bias=bias:
                                e.activation(out, in_, AF.Exp, bias=bias, scale=0.125),
                                reads=[("ps", bank), ("BT",)], writes=[("P", m, b, blk)])
                    if isdiag:
                        blk_ap = Pt[0:kn, lo:lo + tw]
                        mask = self.tri_le[0:kn, 0:tw]
                        self.op("dve", lambda e, blk_ap=blk_ap, mask=mask: e.tensor_tensor(blk_ap, blk_ap, mask, ALU.mult),
                                reads=[("P", m, b, lo // w), ("CB",)], writes=[("P", m, b, lo // w)])

            def stage_av(ki):
                t = ktiles[ki]
                kcol, kn, isdiag, lo = geom(t)
                b = bsel[ki]
                for m in range(2):
                    Pt = self.P[m][b]
                    rhs = Pt[0:kn, lo:N]
                    vl = Vb[0:kn, t * WCA + j * 128:t * WCA + (j + 1) * 128]
                    on = self.ones[0:kn, 0:128]
                    ob, zb = 4 + m, 6 + m
                    oo = self.ps[ob][:, lo:N]
                    zo = self.ps[zb][:, lo:N]
                    st, sp_ = (ki == 0), (ki == nk - 1)

                    def fn(e, oo=oo, zo=zo, vl=vl, on=on, rhs=rhs, st=st, sp_=sp_):
                        e.matmul(oo, vl, rhs, start=st, stop=sp_)
                        return e.matmul(zo, on, rhs, start=st, stop=sp_)
                    self.op("pe", fn, reads=[("P", m, b, q_) for q_ in range(N // w)] + [("V", t), ("CB",)],
                            writes=[("ps", ob), ("ps", zb)])

            bsel = {}
            for step in range(nk + 1):
                if step < nk:
                    stage_qk(step)
                if step >= 1:
                    stage_av(step - 1)
                if step == 2:
                    self.flush_deferred()
            self.flush_deferred()
            RZ0, RZ1, A0, B0 = [x[:, 0:N] for x in TMP]
            O0, O1, Z0, Z1 = [self.ps[i][:, 0:N] for i in (4, 5, 6, 7)]
            self.op("dve", lambda e, RZ0=RZ0, Z0=Z0: e.reciprocal(RZ0, Z0), reads=[("ps", 6)], writes=[("TMP", 0)])
            self.op("dve", lambda e, A0=A0, O0=O0, RZ0=RZ0: e.tensor_tensor(A0, O0, RZ0, ALU.mult),
                    reads=[("ps", 4), ("TMP", 0)], writes=[("TMP", 2)])
            self.op("dve", lambda e, RZ1=RZ1, Z1=Z1: e.reciprocal(RZ1, Z1), reads=[("ps", 7)], writes=[("TMP", 1)])
            self.op("dve", lambda e, B0=B0, O1=O1, RZ1=RZ1: e.tensor_tensor(B0, O1, RZ1, ALU.mult),
                    reads=[("ps", 5), ("TMP", 1)], writes=[("TMP", 3)])
            self.op("dve", lambda e, A0=A0, B0=B0: e.scalar_tensor_tensor(A0, B0, NLAM, A0, ALU.mult, ALU.add),
                    reads=[("TMP", 2), ("TMP", 3), ("LAM",)], writes=[("TMP", 2)])
            self.flush_deferred()
            sq = SQ[:, 0:N]
            ond = ONb[:, q0:q0 + N]
            is_last = (ci == c.NQ - 1)

            def tail(sq=sq, A0=A0, RZ0=RZ0, N=N, ond=ond, is_last=is_last):
                self.op("act", lambda e: e.activation(sq, A0, AF.Square),
                        reads=[("TMP", 2)], writes=[("SQ",)])
                ssb = self.ps[0][:, 0:N]
                on = self.ones[:, 0:128]
                self.op("pe", lambda e: e.matmul(ssb, on, sq, start=True, stop=True),
                        reads=[("SQ",), ("CB",)], writes=[("ps", 0)])
                rs = RZ0
                self.op("act", lambda e: e.activation(rs, ssb, AF.Sqrt, bias=self.EPS, scale=1.0 / 128),
                        reads=[("ps", 0), ("EPS",)], writes=[("TMP", 0)])
                self.op("dve", lambda e: e.reciprocal(rs, rs),
                        reads=[("TMP", 0)], writes=[("TMP", 0)])
                self.op("dve", lambda e: e.scalar_tensor_tensor(ond, A0, GS, rs, ALU.mult, ALU.mult),
                        reads=[("TMP", 2), ("TMP", 0), ("LAM",)], writes=[("ON", on_slot)])
                if is_last and final_cb is not None:
                    final_cb()
            self.deferred = tail

    def sb_head(self, hd, j, on_slot):
        c = self.cfg
        T, WCA = c.T, c.WCA
        QT = self.B[:, self.oQT + j * T:self.oQT + (j + 1) * T]
        KT = self.B[:, self.oKT + j * T:self.oKT + (j + 1) * T]
        Vb = self.B[:, self.oV:self.oV + c.NT * WCA]
        ONb = self.ON[on_slot]
        o = self.oSP
        E1 = [self.Bf32(o + 1024 * i, 512) for i in range(2)]
        SPb = [self.B[:, o + 2048 + 512 * i:o + 2048 + 512 * (i + 1)] for i in range(2)]
        SPACC = [self.B[:, o + 3072 + 512 * i:o + 3072 + 512 * (i + 1)] for i in range(2)]
        P4 = [self.P[0][0], self.P[0][1], self.P[1][0], self.P[1][1]]
        scale = 128.0 ** -0.5
        G = []
        for ci in range(c.NQ):
            q0, N = c.chunk_rng(ci)
            tw = min(128, N)
            if ci == 0:
                ktiles, diag0 = [0], 0
            else:
                diag0 = 4 * (ci - 1) + 1
                ktiles = list(range(4 * ci, 0, -1)) + [0]
            nk = len(ktiles)
            for ki, t in enumerate(ktiles):
                kcol, kn = c.tile_rng(t)
                isdiag = (t >= diag0) if ci > 0 else True
                lo = tw * (t - diag0) if isdiag else 0
                G.append(dict(ci=ci, q0=q0, N=N, tw=tw, ki=ki, nk=nk, t=t, kcol=kcol, kn=kn,
                              isdiag=isdiag, lo=lo, cb=ci % 2, first=(ki == 0), last=(ki == nk - 1)))
        for g, d in enumerate(G):
            d["b"] = g % 2
            d["zb"] = g % 4

        def st_A(d):
            N, lo, kn, zb, cb = d["N"], d["lo"], d["kn"], d["zb"], d["cb"]
            if d["first"]:
                acc_n = SPACC[cb][:, 0:N]
                self.op("dve", lambda e, acc_n=acc_n: e.memset(acc_n, 0.0), reads=(), writes=[("SPACC", cb)])
                Ops = self.ps[6 + cb][:, 0:N]
                zl = self.zeros[:, 0:128]
                zr = QT[:, d["q0"]:d["q0"] + N]
                self.op("pe", lambda e, Ops=Ops, zl=zl, zr=zr: e.matmul(Ops, zl, zr, start=True, stop=False),
                        reads=[("QT", j, d["ci"]), ("CB",)], writes=[("ps", 6 + cb)])
            z = self.ps[zb][0:kn, lo:N]
            lhsT = KT[:, d["kcol"]:d["kcol"] + kn]
            rhs = QT[:, d["q0"] + lo:d["q0"] + N]
            self.op("pe", lambda e, z=z, lhsT=lhsT, rhs=rhs: e.matmul(z, lhsT, rhs, start=True, stop=True),
                    reads=[("QT", j, d["ci"]), ("KT", j, c.chunk_of_tile(d["t"]))], writes=[("ps", zb)])

        def st_B(d):
            N, lo, kn, zb, b, tw = d["N"], d["lo"], d["kn"], d["zb"], d["b"], d["tw"]
            z = self.ps[zb][0:kn, lo:N]
            e1 = E1[b][0:kn, lo:N]
            self.op("act", lambda e, e1=e1, z=z: e.activation(e1, z, AF.Exp, scale=scale),
                    reads=[("ps", zb)], writes=[("E1", b)])
            sp = SPb[b][0:kn, lo:N]
            self.op("act", lambda e, sp=sp, e1=e1: e.activation(sp, e1, AF.Ln, bias=1.0),
                    reads=[("E1", b)], writes=[("SP", b)])
            if d["isdiag"]:
                blk_ap = SPb[b][0:kn, lo:lo + tw]
                mask = self.tri_lt[0:kn, 0:tw]
                self.op("dve", lambda e, blk_ap=blk_ap, mask=mask: e.tensor_tensor(blk_ap, blk_ap, mask, ALU.mult),
                        reads=[("SP", b), ("CB",)], writes=[("SP", b)])

        def st_D(d):
            N, lo, kn, zb, b, cb = d["N"], d["lo"], d["kn"], d["zb"], d["b"], d["cb"]
            z = self.ps[zb][0:kn, lo:N]
            sp = SPb[b][0:kn, lo:N]
            nti = self.ntincl[0:kn, 0:kn]
            non = self.nones[:, 0:kn]
            acc = SPACC[cb][:, lo:N]
            first = d["first"]

            def fn(e, z=z, nti=nti, sp=sp, non=non, acc=acc, first=first):
                ins = e.matmul(z, nti, sp, start=False, stop=first, skip_group_check=True)
                if not first:
                    ins = e.matmul(z, non, acc, start=False, stop=True, skip_group_check=True)
                return ins
            self.op("pe", fn, reads=[("SP", b), ("CB",)] + ([] if first else [("SPACC", cb)]), writes=[("ps", zb)])
            if not d["last"]:
                accw = SPACC[cb][0:kn, lo:N]
                self.op("dve", lambda e, accw=accw, sp=sp: e.tensor_tensor(accw, accw, sp, ALU.add),
                        reads=[("SP", b), ("SPACC", cb)], writes=[("SPACC", cb)])

        def st_E(d):
            N, lo, kn, zb, tw = d["N"], d["lo"], d["kn"], d["zb"], d["tw"]
            z = self.ps[zb][0:kn, lo:N]
            p = P4[zb][0:kn, lo:N]
            self.op("act", lambda e, p=p, z=z: e.activation(p, z, AF.Exp, scale=scale),
                    reads=[("ps", zb)], writes=[("PS", zb)])
            if d["isdiag"]:
                blk_ap = P4[zb][0:kn, lo:lo + tw]
                mask = self.tri_lt[0:kn, 0:tw]
                self.op("dve", lambda e, blk_ap=blk_ap, mask=mask: e.tensor_tensor(blk_ap, blk_ap, mask, ALU.mult),
                        reads=[("PS", zb), ("CB",)], writes=[("PS", zb)])

        def st_F(d):
            N, lo, kn, zb, cb, t = d["N"], d["lo"], d["kn"], d["zb"], d["cb"], d["t"]
            p = P4[zb][0:kn, lo:N]
            vl = Vb[0:kn, t * WCA + j * 128:t * WCA + (j + 1) * 128]
            oo = self.ps[6 + cb][:, lo:N]
            self.op("pe", lambda e, oo=oo, vl=vl, p=p, last=d["last"]: e.matmul(oo, vl, p, start=False, stop=last),
                    reads=[("PS", zb), ("V", t)], writes=[("ps", 6 + cb)])
            if d["last"]:
                ond = ONb[:, d["q0"]:d["q0"] + N]
                Ops = self.ps[6 + cb][:, 0:N]
                self.copy_op("dve", ond, Ops, reads=[("ps", 6 + cb)], writes=[("ON", on_slot)])

        ng = len(G)
        for step in range(ng + 3):
            if step < ng:
                st_A(G[step])
            if 1 <= step <= ng:
                st_D(G[step - 1])
            if step < ng:
                st_B(G[step])
            if 2 <= step <= ng + 1:
                st_E(G[step - 2])
            if 3 <= step:
                st_F(G[step - 3])

    def store_on(self, hd, on_slot):
        dst = self.dt["ons"][hd]
        src = self.ON[on_slot][:, :]
        self.op("sp", lambda e: e.dma_start(out=dst, in_=src), reads=[("ON", on_slot)], writes=[("ons", hd)], dma=True)

    def build_seq(self, seq):
        c = self.cfg
        D, T, NC, NT, G, WCA, WC = c.D, c.T, c.NC, c.NT, c.G, c.WCA, c.WC
        dt = self.dt
        rec = self.rec
        h = dt["h"]
        self.op("sp", lambda e: e.dma_start(out=h[0:NMETA, :], in_=dt["meta"]),
                reads=(), writes=[("h", 0, q) for q in range(NC)], dma=True)
        for t in range(1, NT):
            ts, n = c.tile_rng(t)
            src = dt["x"][seq, ts - NMETA:ts - NMETA + n, :]
            dst = h[ts:ts + n, :]
            self.op("sp", lambda e, dst=dst, src=src: e.dma_start(out=dst, in_=src),
                    reads=(), writes=[("h", t, q) for q in range(NC)], dma=True)

        A = self.A
        QTb = self.B[:, self.oQT:self.oQT + G * T]
        KTb = self.B[:, self.oKT:self.oKT + G * T]
        Vb = self.B[:, self.oV:self.oV + NT * WCA]
        jobs = []
        jtags = []
        self.jtag = "init"

        class _JL(list):
            def append(jl, x):
                list.append(jl, x)
                jtags.append(self.jtag)
        jobs = _JL()

        def akeys_chunk(ci):
            return self.a_keys_chunk(ci)

        def akeys_tile(t):
            return [("A", k, t) for k in range(NC)]

        def evac_to(buf, keyname):
            def ev(j, ci, ps_ap, bank):
                cs, n = c.chunk_rng(ci)
                dst = buf[:, j * T + cs:j * T + cs + n]
                self.copy_op(self.evac_engine(), dst, ps_ap, reads=[("ps", bank)], writes=[(keyname, j, ci)])
            return ev

        def evac_v(t, ps_ap, bank):
            ts, n = c.tile_rng(t)
            dst = Vb[0:n, t * WCA:(t + 1) * WCA]
            self.copy_op(self.evac_engine(), dst, ps_ap, reads=[("ps", bank)], writes=[("V", t)])

        def job_gF(src, ncols, evac):
            jobs.append((src, ncols, lambda wv, wk: self.gemm_F(A, akeys_chunk, wv, wk, ncols, evac)))

        def job_gT(src, ncols, evac, pre=None):
            jobs.append((src, ncols, lambda wv, wk: self.gemm_T(A, akeys_tile, wv, wk, ncols, evac, pre)))

        def job(fn):
            jobs.append((None, 0, lambda wv, wk: fn()))

        def load_on_to_A(heads=None):
            for k in (range(NC) if heads is None else heads):
                dst = A[:, k * T:(k + 1) * T]
                src = dt["ons"][k]
                self.op("sp", lambda e, dst=dst, src=src: e.dma_start(out=dst, in_=src),
                        reads=[("ons", k)], writes=[("A", k, t) for t in range(NT)], dma=True)

        def o_proj(w_ap):
            self.jtag = "oproj"
            job(lambda: load_on_to_A(range((c.NG - 1) * G, NC)))
            for cbk in range(D // WC):
                pre, ev = self.make_rmw(cbk * WC, WC)
                job_gT(w_ap[:, cbk * WC:(cbk + 1) * WC], WC, ev, pre)

        def mlp(layer):
            self.jtag = "norm"
            job(lambda: self.norm(dt["mlp_g"][layer]))
            H1 = self.B
            for qtr in range(4):
                for ft in range(D // WC):
                    f0 = qtr * D + ft * WC

                    def ev_up(j, ci, ps_ap, bank, ft=ft):
                        cs, n = c.chunk_rng(ci)
                        fc = ft * (WC // 128) + j
                        dst = H1[:, fc * T + cs:fc * T + cs + n]
                        ri = self.rt_i = (getattr(self, "rt_i", 0) + 1) % 2
                        rt = self.RT[ri][:, 0:n]
                        self.op("act", lambda e, rt=rt, ps_ap=ps_ap: e.activation(rt, ps_ap, AF.Relu),
                                reads=[("ps", bank)], writes=[("RT", ri)])
                        self.op("dve", lambda e, dst=dst, rt=rt: e.tensor_tensor(dst, rt, rt, ALU.mult),
                                reads=[("RT", ri)], writes=[("H1", fc, ci)])
                    self.jtag = "mlp_up"
                    job_gF(dt["wup"][layer][:, f0:f0 + WC], WC, ev_up)
                for cbk in range(D // WC):
                    pre, ev = self.make_rmw(cbk * WC, WC)

                    def gT(wv, wk, ev=ev, pre=pre):
                        self.gemm_T(H1, lambda t: [("H1", fc, c.chunk_of_tile(t)) for fc in range(NC)], wv, wk, WC, ev, pre)
                    self.jtag = "mlp_down"
                    jobs.append((dt["wdown"][layer][qtr * D:(qtr + 1) * D, cbk * WC:(cbk + 1) * WC], WC, gT))

        for l in range(c.NA):
            lam_init = 0.8 - 0.6 * math.exp(-0.3 * l)
            self.jtag = "norm"
            job(lambda l=l: self.norm(dt["attn_g"][l]))
            NLAM = self.SM[:, 260:261]
            GS = self.SM[:, 261:262]

            def lam_setup(l=l, lam_init=lam_init):
                SM = self.SM
                for i, nm in enumerate(("lq1", "lk1", "lq2", "lk2")):
                    dst = SM[:, 64 * i:64 * (i + 1)]
                    src = dt[nm][l].partition_broadcast(128)
                    self.op("sp", lambda e, dst=dst, src=src: e.dma_start(out=dst, in_=src), reads=(), writes=[("LQ", i)], dma=True)
                gs_raw = SM[:, 262:263]
                src = dt["subln"][l:l + 1, :].rearrange("a p -> p a")
                self.op("sp", lambda e: e.dma_start(out=gs_raw, in_=src), reads=(), writes=[("GSR",)], dma=True)
                pr = [SM[:, 264 + 2 * i:265 + 2 * i] for i in range(2)]
                for i in range(2):
                    a = SM[:, 128 * i:128 * i + 64]
                    b_ = SM[:, 128 * i + 64:128 * i + 128]
                    self.op("dve", lambda e, a=a, b_=b_: e.tensor_tensor(a, a, b_, ALU.mult),
                            reads=[("LQ", 2 * i), ("LQ", 2 * i + 1)], writes=[("LQ", 2 * i)])
                    self.op("dve", lambda e, a=a, p_=pr[i]: e.tensor_reduce(p_, a, AX.X, ALU.add),
                            reads=[("LQ", 2 * i)], writes=[("PR", i)])
                    self.op("act", lambda e, p_=pr[i]: e.activation(p_, p_, AF.Exp), reads=[("PR", i)], writes=[("PR", i)])
                self.op("dve", lambda e: e.tensor_tensor(NLAM, pr[1], pr[0], ALU.subtract),
                        reads=[("PR", 0), ("PR", 1)], writes=[("LAM",)])
                self.op("dve", lambda e: e.tensor_scalar(NLAM, NLAM, -lam_init, None, ALU.add),
                        reads=[("LAM",)], writes=[("LAM",)])
                self.op("dve", lambda e: e.tensor_scalar(GS, gs_raw, 1.0 - lam_init, None, ALU.mult),
                        reads=[("GSR",), ("LAM",)], writes=[("LAM",)])
            job(lam_setup)
            for g in range(c.NG):
                wq = dt["wqkv"][l][:, g * WCA:(g + 1) * WCA]
                wk_ = dt["wqkv"][l][:, D + g * WCA:D + (g + 1) * WCA]
                wv_ = dt["wqkv"][l][:, 2 * D + g * WCA:2 * D + (g + 1) * WCA]
                self.jtag = "qkv"
                job_gF(wq, WCA, evac_to(QTb, "QT"))
                job_gF(wk_, WCA, evac_to(KTb, "KT"))
                job_gT(wv_, WCA, evac_v)
                if g == c.NG - 1 and g > 0:
                    self.jtag = "qkv"
                    job(lambda: load_on_to_A(range(0, (c.NG - 1) * G)))
                for j in range(G):
                    hd = g * G + j

                    def att(l=l, hd=hd, j=j, NLAM=NLAM, GS=GS):
                        slot = hd % 2
                        self.da_head(l, hd, j, slot, NLAM, GS, final_cb=lambda: self.store_on(hd, slot))
                    self.jtag = "da_att"
                    job(att)
            o_proj(dt["wo_a"][l])
            mlp(l)

        if c.NB > 0:
            self.jtag = "norm"
            job(lambda: self.norm(dt["kvg"][0]))
            self.jtag = "kv"
            for g in range(c.NG):
                def ev_k(j, ci, ps_ap, bank):
                    cs, n = c.chunk_rng(ci)
                    dst = KTb[:, j * T + cs:j * T + cs + n]
                    self.copy_op(self.evac_engine(), dst, ps_ap, reads=[("ps", bank)], writes=[("KT", j, ci)])
                job_gF(dt["wk"][:, g * WCA:(g + 1) * WCA], WCA, ev_k)

                def st_k(g=g):
                    dst = dt["ks"][g * G:(g + 1) * G].rearrange("h p t -> p h t")
                    src = KTb.rearrange("p (h t) -> p h t", h=G)
                    self.op("sp", lambda e: e.dma_start(out=dst, in_=src),
                            reads=[("KT", j, ci) for j in range(G) for ci in range(c.NQ)], writes=[("ks", g)], dma=True)
                job(st_k)
                job_gT(dt["wv"][:, g * WCA:(g + 1) * WCA], WCA, evac_v)

                def st_v(g=g):
                    vs = dt["vs"]
                    self.op("sp", lambda e: e.dma_start(out=vs[0:NMETA, g * WCA:(g + 1) * WCA], in_=Vb[0:NMETA, 0:WCA]),
                            reads=[("V", 0)], writes=[("vs", g, 0)], dma=True)
                    dst = vs[NMETA:, g * WCA:(g + 1) * WCA].rearrange("(t p) c -> p t c", p=128)
                    src = Vb[:, WCA:].rearrange("p (t c) -> p t c", c=WCA)
                    self.op("sp", lambda e: e.dma_start(out=dst, in_=src),
                            reads=[("V", t) for t in range(1, NT)], writes=[("vs", g, 1)], dma=True)
                job(st_v)

        for jb in range(c.NB):
            l = c.NA + jb
            self.jtag = "norm"
            job(lambda l=l: self.norm(dt["attn_g"][l]))
            for g in range(c.NG):
                self.jtag = "q_sb"
                job_gF(dt["wq_b"][jb][:, g * WCA:(g + 1) * WCA], WCA, evac_to(QTb, "QT"))

                def ld_kv(g=g):
                    src = dt["ks"][g * G:(g + 1) * G].rearrange("h p t -> p h t")
                    dst = KTb.rearrange("p (h t) -> p h t", h=G)
                    self.op("sp", lambda e: e.dma_start(out=dst, in_=src),
                            reads=[("ks", g)], writes=[("KT", j, ci) for j in range(G) for ci in range(c.NQ)], dma=True)
                    vs = dt["vs"]
                    self.op("sp", lambda e: e.dma_start(out=Vb[0:NMETA, 0:WCA], in_=vs[0:NMETA, g * WCA:(g + 1) * WCA]),
                            reads=[("vs", g, 0)], writes=[("V", 0)], dma=True)
                    src2 = vs[NMETA:, g * WCA:(g + 1) * WCA].rearrange("(t p) c -> p t c", p=128)
                    dst2 = Vb[:, WCA:].rearrange("p (t c) -> p t c", c=WCA)
                    self.op("sp", lambda e: e.dma_start(out=dst2, in_=src2),
                            reads=[("vs", g, 1)], writes=[("V", t) for t in range(1, NT)], dma=True)
                job(ld_kv)
                if g == c.NG - 1 and g > 0:
                    job(lambda: load_on_to_A(range(0, (c.NG - 1) * G)))
                for j in range(G):
                    hd = g * G + j

                    def att(hd=hd, j=j):
                        slot = hd % 2
                        self.sb_head(hd, j, slot)
                        self.store_on(hd, slot)
                    self.jtag = "sb_att"
                    job(att)
            o_proj(dt["wo_b"][jb])
            mlp(l)

        self.jtag = "norm"
        job(lambda: self.norm(dt["fng"][0], final_seq=seq))

        widx = [i for i, jb_ in enumerate(jobs) if jb_[0] is not None]
        loaded = {}
        nxt = [0]

        owner = self.hs_owner
        done = self.w_done

        def ensure(upto_pos):
            while nxt[0] < len(widx) and nxt[0] <= upto_pos:
                i = widx[nxt[0]]
                src, ncols, _ = jobs[i]
                hs, nh, _p = self.peek_w(ncols > 256)
                if any(owner[hs + q] is not None and owner[hs + q] not in done for q in range(nh)):
                    break
                for q in range(nh):
                    owner[hs + q] = (seq, i)
                loaded[i] = self.load_w(src, ncols)
                nxt[0] += 1

        wpos = {i: p for p, i in enumerate(widx)}
        cur = 0
        for i, (src, ncols, fn) in enumerate(jobs):
            self.rec.tag = (seq, jtags[i])
            if jtags[i] != "da_att":
                self.flush_deferred()
            if src is not None:
                p = wpos[i]
                depth = 1 if ncols > 256 else 2
                ensure(p + depth)
                cur = p + 1
                wv, wk = loaded.pop(i)
                fn(wv, wk)
                done.add((seq, i))
            else:
                ensure(cur)
                fn(None, None)

    def bt_table(self):
        c = self.cfg
        bt = np.zeros((128, self.NBT), np.float32)
        p = np.arange(128, dtype=np.float64)
        for (hd, delta), col in self.bt_cols.items():
            bt[:, col] = (alibi_slope(hd, c.NC) * (p + delta)).astype(np.float32)
        return bt


def const_cb16():
    p = np.arange(128)[:, None]
    j = np.arange(128)[None, :]
    ident = (p == j)
    ones = np.ones((128, 128), bool)
    tri_le = (p <= j)
    tri_lt = (p < j)
    tincl = (p >= j)
    cb = np.concatenate([ident, ones, tri_le, tri_lt, tincl, np.zeros((128, 128), bool)], axis=1).astype(np.float32)
    neg = -float(np.sqrt(128.0))
    cb = np.concatenate([cb, neg * tincl.astype(np.float32), neg * np.ones((128, 128), np.float32)], axis=1)
    return cb.astype(ml_dtypes.bfloat16)


_CACHE = {}


def build_program(cfg, debug=False):
    nc = bass.Bass("TRN2", target_bir_lowering=False)
    b = Builder(cfg, debug)
    b.rec.max_ops = getattr(cfg, "max_ops", None)
    with ExitStack() as es:
        b.declare(nc)
        b.allocate(nc, es)
        b.op("sp", lambda e: e.dma_start(out=b.CB[:, :], in_=b.dt["cb16"]), reads=(), writes=[("CB",)], dma=True)
        b.op("sp", lambda e: e.dma_start(out=b.BT[:, :], in_=b.dt["bt"]), reads=(), writes=[("BT",)], dma=True)
        b.op("dve", lambda e: e.memset(b.EPS, RMS_EPS), reads=(), writes=[("EPS",)])
        for seq in range(cfg.NSEQ):
            b.build_seq(seq)
        b.rec.max_ops = None
        b.rec.barrier()
        b.op("sp", None, reads=(), writes=())
        print("n_ops", len(b.rec.ops), flush=True)
        sems = {}
        dsems = {}
        for e in ENGINES:
            sems[e] = [es.enter_context(nc.semaphore(f"s_{e}{i}")) for i in range(SEM_RING)]
        for q, K in DMA_RING.items():
            dsems[q] = [es.enter_context(nc.semaphore(f"d_{q}{i}")) for i in range(K)]
        with nc.Block() as block:
            b.rec.emit(nc, block, sems, dsems)
    return nc, b


def run_cfg(cfg, inputs, n_cores=8, trace=False):
    nc, b = build_program(cfg)
    f32 = lambda a: np.ascontiguousarray(np.asarray(a, dtype=np.float32))
    shared = {
        "meta": f32(inputs["meta_tokens"]),
        "attn_g": f32(inputs["attn_norm_g"]),
        "mlp_g": f32(inputs["mlp_norm_g"]),
        "wqkv": f32(inputs["da_w_qkv"]),
        "wo_a": f32(inputs["da_w_o"]),
        "lq1": f32(inputs["da_lambda_q1"]),
        "lk1": f32(inputs["da_lambda_k1"]),
        "lq2": f32(inputs["da_lambda_q2"]),
        "lk2": f32(inputs["da_lambda_k2"]),
        "subln": f32(inputs["da_subln_g"]),
        "kvg": f32(inputs["kv_norm_g"]).reshape(1, -1),
        "wk": f32(inputs["sb_w_k"]),
        "wv": f32(inputs["sb_w_v"]),
        "wq_b": f32(inputs["sb_w_q"]),
        "wo_b": f32(inputs["sb_w_o"]),
        "wup": f32(inputs["mlp_w_up"]),
        "wdown": f32(inputs["mlp_w_down"]),
        "fng": f32(inputs["final_norm_g"]).reshape(1, -1),
        "cb16": const_cb16(),
        "bt": b.bt_table(),
    }
    x = f32(inputs["x"])
    in_maps = []
    for core in range(n_cores):
        m = dict(shared)
        m["x"] = np.ascontiguousarray(x[core * cfg.NSEQ:(core + 1) * cfg.NSEQ])
        in_maps.append(m)
    res = run_bass_kernel_spmd(nc, in_maps, core_ids=list(range(n_cores)), trace=trace)
    out = np.concatenate([np.asarray(r["out"]) for r in res.results], axis=0)
    return out.astype(np.float32), res


def kernel(**inputs):
    cfg = Cfg(D=2048, S=2048, NSEQ=2, NA=2, NB=2)
    out, _ = run_cfg(cfg, inputs, n_cores=8)
    return out
```

```python
import math
from contextlib import ExitStack

import numpy as np
import ml_dtypes

import concourse.bass as bass
import concourse.mybir as mybir
from concourse.bass_utils import run_bass_kernel_spmd

F32 = mybir.dt.float32
BF16 = mybir.dt.bfloat16
AF = mybir.ActivationFunctionType
ALU = mybir.AluOpType
AX = mybir.AxisListType

NMETA = 16
RMS_EPS = 1e-6
ENGINES = ("pe", "act", "dve", "pool", "sp")
SEM_RING = 8
DMA_RING = {"sp": 40, "pool": 16}


class Cfg:
    def __init__(self, D=2048, S=2048, NSEQ=2, NA=2, NB=2, force_w=None):
        self.D, self.S, self.NSEQ, self.NA, self.NB = D, S, NSEQ, NA, NB
        self.DEPTH = NA + NB
        self.NC = D // 128
        self.F = 4 * D
        self.T = S + NMETA
        self.NT = S // 128 + 1
        self.NQ = S // 512 + 1
        self.G = 2
        self.WCA = 128 * self.G
        self.WC = min(512, D)
        self.NG = self.NC // self.G
        self.force_w = force_w

    def tile_rng(self, t):
        return (0, NMETA) if t == 0 else (NMETA + 128 * (t - 1), 128)

    def chunk_rng(self, c):
        return (0, NMETA) if c == 0 else (NMETA + 512 * (c - 1), 512)

    def tiles_of_chunk(self, c):
        return [0] if c == 0 else list(range(4 * (c - 1) + 1, 4 * c + 1))

    def chunk_of_tile(self, t):
        return 0 if t == 0 else (t - 1) // 4 + 1


class _Op:
    __slots__ = ("eng", "fn", "deps", "dma", "signum", "dsem", "dval", "need", "tag", "nmm")

    def __init__(self, eng, fn, deps, dma):
        self.eng, self.fn, self.deps, self.dma = eng, fn, deps, dma
        self.signum = None
        self.dsem = None
        self.dval = None
        self.need = False
        self.tag = None
        self.nmm = 0


class _CntProxy:
    def __init__(self, e):
        self.e = e
        self.n = 0

    def matmul(self, *a, **k):
        self.n += 1
        return self.e.matmul(*a, **k)

    def transpose(self, *a, **k):
        self.n += 1
        return self.e.transpose(*a, **k)


class Rec:
    def __init__(self):
        self.ops = []
        self.lastw = {}
        self.rd = {}
        self.last_on = {}
        self.unobs = {"sp": set(), "pool": set()}
        self.bar_deps = set()
        self.bar_pending = set()

    max_ops = None
    tag = None

    def op(self, eng, fn, reads=(), writes=(), dma=False, nobar=False, force=False):
        if self.max_ops is not None and len(self.ops) >= self.max_ops and not force:
            return None
        idx = len(self.ops)
        psr = [k for k in reads if k[0] == "ps"]
        if psr:
            reads = [k for k in reads if k[0] != "ps"]
            writes = list(writes) + [k for k in psr if k not in writes]
        deps = set()
        for k in reads:
            w = self.lastw.get(k)
            if w is not None:
                deps.add(w)
        for k in writes:
            w = self.lastw.get(k)
            if w is not None:
                deps.add(w)
            r = self.rd.get(k)
            if r:
                for v in r.values():
                    if isinstance(v, list):
                        deps.update(v)
                    else:
                        deps.add(v)
        if not nobar and eng in self.bar_pending:
            deps |= self.bar_deps
            self.bar_pending.discard(eng)
        for k in reads:
            r = self.rd.setdefault(k, {})
            if dma:
                r.setdefault(("dma", eng), []).append(idx)
            else:
                r[eng] = idx
        for k in writes:
            self.lastw[k] = idx
            self.rd[k] = {}
        deps.discard(idx)
        for d in deps:
            od = self.ops[d]
            if od.dma:
                self.unobs[od.eng].discard(d)
        self.ops.append(_Op(eng, fn, deps, dma))
        self.ops[-1].tag = self.tag
        if not nobar:
            self.last_on[eng] = idx
        if dma and not nobar:
            self.unobs[eng].add(idx)
        return idx

    def barrier(self):
        for q in ("sp", "pool"):
            un = set(self.unobs[q])
            if un:
                idx = len(self.ops)
                self.ops.append(_Op(q, None, un, False))
                self.last_on[q] = idx
                self.unobs[q] = set()
        self.bar_deps = set(self.last_on.values())
        self.bar_pending = set(ENGINES)

    def emit(self, nc, block, sems, dsems):
        ops = self.ops
        for o in ops:
            for d in o.deps:
                ops[d].need = True
        cnt = {e: 0 for e in ENGINES}
        dcnt = {"sp": 0, "pool": 0}
        for o in ops:
            if o.dma:
                j = dcnt[o.eng]
                dcnt[o.eng] += 1
                K = len(dsems[o.eng])
                o.dsem = dsems[o.eng][j % K]
                o.dval = 16 * (j // K + 1)
            elif o.need:
                cnt[o.eng] += 1
                o.signum = cnt[o.eng]
        per_eng = {e: [] for e in ENGINES}
        for i, o in enumerate(ops):
            per_eng[o.eng].append(i)

        def run(ename, eng):
            waited = {e: 0 for e in ENGINES}
            dwaited = {}
            for i in per_eng[ename]:
                o = ops[i]
                need_c = {}
                for d in o.deps:
                    od = ops[d]
                    if od.dma:
                        key = id(od.dsem)
                        if dwaited.get(key, 0) < od.dval:
                            dwaited[key] = od.dval
                            eng.wait_ge(od.dsem, od.dval)
                    else:
                        if od.eng == "pe" and ename == "pe":
                            continue
                        if od.signum > need_c.get(od.eng, 0):
                            need_c[od.eng] = od.signum
                for e2, sn in need_c.items():
                    if sn > waited[e2]:
                        waited[e2] = sn
                        eng.wait_ge(sems[e2][(sn - 1) % SEM_RING], (sn - 1) // SEM_RING + 1)
                if ename == "pe" and o.fn is not None:
                    cp = _CntProxy(eng)
                    ins = o.fn(cp)
                    o.nmm = cp.n
                else:
                    ins = o.fn(eng) if o.fn is not None else None
                if o.dma:
                    ins.then_inc(o.dsem, 16)
                elif o.signum is not None:
                    s = sems[ename][(o.signum - 1) % SEM_RING]
                    if ins is None:
                        eng.sem_inc(s, 1)
                    else:
                        ins.then_inc(s, 1)

        @block.tensor
        def _(e):
            run("pe", e)

        @block.scalar
        def _(e):
            run("act", e)

        @block.vector
        def _(e):
            run("dve", e)

        @block.gpsimd
        def _(e):
            run("pool", e)

        @block.sync
        def _(e):
            run("sp", e)


def alibi_slope(h, nh):
    return 2.0 ** (-8.0 * (h + 1) / nh)


class Builder:
    def __init__(self, cfg, debug=False):
        self.cfg = cfg
        self.debug = debug
        self.rec = Rec()
        self.bt_cols = {}
        self.gemm_bank = 0
        self.wptr = 0
        self.hs_owner = [None] * 4
        self.da_cnt = 0
        self.deferred = None
        self.w_done = set()
        self.evac_flip = 0

    def declare(self, nc):
        c = self.cfg
        D, S, T, NC, F = c.D, c.S, c.T, c.NC, c.F
        dt = {}

        def inp(name, shape, dtype=F32):
            dt[name] = nc.dram_tensor(name, list(shape), dtype, kind="ExternalInput").ap()

        inp("x", [c.NSEQ, S, D])
        inp("meta", [NMETA, D])
        inp("attn_g", [c.DEPTH, D])
        inp("mlp_g", [c.DEPTH, D])
        inp("wqkv", [c.NA, D, 3 * D])
        inp("wo_a", [c.NA, D, D])
        inp("lq1", [c.NA, 64])
        inp("lk1", [c.NA, 64])
        inp("lq2", [c.NA, 64])
        inp("lk2", [c.NA, 64])
        inp("subln", [c.NA, 128])
        inp("kvg", [1, D])
        inp("wk", [D, D])
        inp("wv", [D, D])
        inp("wq_b", [c.NB, D, D])
        inp("wo_b", [c.NB, D, D])
        inp("wup", [c.DEPTH, D, F])
        inp("wdown", [c.DEPTH, F, D])
        inp("fng", [1, D])
        inp("cb16", [128, 8 * 128], BF16)
        self.NBT = NC * 96
        inp("bt", [128, self.NBT])
        dt["out"] = nc.dram_tensor("out", [c.NSEQ, S, D], F32, kind="ExternalOutput").ap()
        dt["h"] = nc.dram_tensor("h_scr", [T, D], F32).ap()
        dt["ons"] = nc.dram_tensor("ons_scr", [NC, 128, T], BF16).ap()
        dt["ks"] = nc.dram_tensor("ks_scr", [NC, 128, T], BF16).ap()
        dt["vs"] = nc.dram_tensor("vs_scr", [T, D], BF16).ap()
        self.dt = dt

    def allocate(self, nc, es):
        c = self.cfg
        D, T, NC, NT, G = c.D, c.T, c.NC, c.NT, c.G

        def sb(name, cols, dtype):
            return es.enter_context(nc.sbuf_tensor(name, [128, cols], dtype))

        self.A = sb("A", NC * T, BF16)
        self.oQT = 0
        self.oKT = G * T
        self.oV = 2 * G * T
        self.oSP = 2 * G * T + NT * c.WCA
        self.oSP += self.oSP % 2
        spare = max(8 * D, 8 * 1024 + 2048)
        self.Bcols = max(NC * T, self.oSP + spare)
        self.Bcols += self.Bcols % 2
        self.B = sb("B", self.Bcols, BF16)
        self.Wt = sb("Wt", 4 * NC * 256, BF16)
        self.RBALL = sb("RBALL", 8 * 512, F32)
        self.RB = [self.RBALL[:, i * 512:(i + 1) * 512] for i in range(8)]
        self.ON = [sb(f"ON{i}", T, BF16) for i in range(2)]
        self.P = [[sb(f"P{m}{b}", 512, BF16) for b in range(2)] for m in range(2)]
        self.CB = sb("CB", 8 * 128, BF16)
        self.BT = sb("BT", self.NBT, F32)
        self.SM = sb("SM", 4 * 64 + 64, F32)
        self.EPS = self.SM[:, 270:271]
        self.RT = [sb(f"RT{i}", 512, F32) for i in range(2)]
        self.ps = [es.enter_context(nc.psum_tensor(f"ps{i}", [128, 512], F32)) for i in range(8)]
        self.ident = self.CB[:, 0:128]
        self.ones = self.CB[:, 128:256]
        self.tri_le = self.CB[:, 256:384]
        self.tri_lt = self.CB[:, 384:512]
        self.tincl = self.CB[:, 512:640]
        self.zeros = self.CB[:, 640:768]
        self.ntincl = self.CB[:, 768:896]
        self.nones = self.CB[:, 896:1024]

    def Bf32(self, off_bf16, cols_f32):
        return self.B[:, off_bf16:off_bf16 + 2 * cols_f32].bitcast(F32)

    def op(self, *a, **k):
        return self.rec.op(*a, **k)

    def next_bank(self):
        b = self.gemm_bank % 4
        self.gemm_bank += 1
        return b

    def peek_w(self, big):
        p = self.wptr
        if big:
            if p % 2:
                p += 1
            return p % 4, 2, p + 2
        return p % 4, 1, p + 1

    def alloc_w(self, big):
        hs, nh, p = self.peek_w(big)
        self.wptr = p
        return hs, nh

    def load_w(self, src, ncols):
        c = self.cfg
        NC = c.NC
        big = ncols > 256
        hs, nh = self.alloc_w(big)
        base = hs * NC * 256
        view = self.Wt[:, base:base + NC * ncols].rearrange("p (k c) -> p k c", k=NC)
        nsplit = 4 if NC >= 4 else 1
        kper = NC // nsplit
        keys = []
        hkeys = [("W", hs + i) for i in range(nh)]
        for s in range(nsplit):
            key = ("Wp", hs, s)
            keys.append(key)
            srcv = src[s * kper * 128:(s + 1) * kper * 128, :].rearrange("(k p) c -> p k c", p=128)
            dstv = view[:, s * kper:(s + 1) * kper, :]

            def fn(e, dstv=dstv, srcv=srcv):
                return e.dma_start(out=dstv, in_=srcv)
            self.op("pool", fn, reads=(), writes=[key] + hkeys, dma=True, nobar=True)
        return view, keys + hkeys

    def evac_engine(self):
        self.evac_flip ^= 1
        return "act" if self.evac_flip else "dve"

    def copy_op(self, eng, out, in_, reads, writes):
        if eng == "act":
            def fn(e):
                return e.copy(out, in_)
        else:
            def fn(e):
                return e.tensor_copy(out, in_)
        self.op(eng, fn, reads, writes)

    def a_keys_chunk(self, cidx):
        c = self.cfg
        return [("A", k, t) for k in range(c.NC) for t in c.tiles_of_chunk(cidx)]

    def gemm_F(self, X, xkeys_fn, wview, wkeys, ncols, evac):
        c = self.cfg
        NC, T = c.NC, c.T
        for j in range(ncols // 128):
            for ci in range(c.NQ):
                cs, n = c.chunk_rng(ci)
                bank = self.next_bank()
                ps = self.ps[bank]
                pairs = [(wview[:, k, j * 128:(j + 1) * 128], X[:, k * T + cs:k * T + cs + n]) for k in range(NC)]
                out = ps[:, 0:n]

                def fn(e, pairs=pairs, out=out):
                    nn = len(pairs)
                    for i, (l, r) in enumerate(pairs):
                        ins = e.matmul(out, l, r, start=(i == 0), stop=(i == nn - 1))
                    return ins
                self.op("pe", fn, reads=list(wkeys) + xkeys_fn(ci), writes=[("ps", bank)])
                evac(j, ci, ps[:, 0:n], bank)

    def gemm_T(self, X, xkeys_fn, wview, wkeys, ncols, evac, pre=None):
        c = self.cfg
        NC, T = c.NC, c.T
        PF = 4
        if pre is not None:
            for t in range(min(PF, c.NT)):
                pre(t)
        for t in range(c.NT):
            ts, n = c.tile_rng(t)
            if pre is not None and t + PF < c.NT:
                pre(t + PF)
            bank = self.next_bank()
            ps = self.ps[bank]
            pairs = [(X[:, k * T + ts:k * T + ts + n], wview[:, k, :]) for k in range(NC)]
            out = ps[0:n, 0:ncols]

            def fn(e, pairs=pairs, out=out):
                nn = len(pairs)
                for i, (l, r) in enumerate(pairs):
                    ins = e.matmul(out, l, r, start=(i == 0), stop=(i == nn - 1))
                return ins
            self.op("pe", fn, reads=list(wkeys) + xkeys_fn(t), writes=[("ps", bank)])
            evac(t, out, bank)

    def make_rmw(self, col0, ncols):
        c = self.cfg
        h = self.dt["h"]
        state = {"i": 0, "slot": {}}

        def pre(t):
            ts, n = c.tile_rng(t)
            s = state["i"] % 8
            state["i"] += 1
            state["slot"][t] = s
            rb = self.RB[s]
            src = h[ts:ts + n, col0:col0 + ncols]
            dst = rb[0:n, 0:ncols]
            self.op("sp", lambda e: e.dma_start(out=dst, in_=src),
                    reads=[("h", t, col0 // 128 + q) for q in range(ncols // 128)], writes=[("RB", s)], dma=True)

        def evac(t, ps_ap, bank):
            ts, n = c.tile_rng(t)
            s = state["slot"][t]
            rb = self.RB[s][0:n, 0:ncols]
            self.op("dve", lambda e: e.tensor_tensor(rb, rb, ps_ap, ALU.add),
                    reads=[("ps", bank), ("RB", s)], writes=[("RB", s)])
            dst = h[ts:ts + n, col0:col0 + ncols]
            self.op("sp", lambda e: e.dma_start(out=dst, in_=rb),
                    reads=[("RB", s)], writes=[("h", t, col0 // 128 + q) for q in range(ncols // 128)], dma=True)
        return pre, evac

    def norm(self, g_ap, final_seq=None):
        c = self.cfg
        D, NC, T = c.D, c.NC, c.T
        rec = self.rec
        rec.barrier()
        o = self.oSP
        HT = [self.Bf32(o + 2 * D * i, D) for i in range(3)]
        HN = [self.B[:, o + 6 * D:o + 7 * D], self.B[:, o + 7 * D:o + 8 * D]]
        assert D <= 4096
        GT = self.RBALL[:, 0:D]
        SSQ = [self.SM[:, 256 + 2 * s:256 + 2 * s + 1] for s in range(2)]
        RSTD = [self.SM[:, 257 + 2 * s:257 + 2 * s + 1] for s in range(2)]
        EPS = self.EPS
        h = self.dt["h"]
        self.op("sp", lambda e: e.dma_start(out=GT, in_=g_ap.partition_broadcast(128)),
                reads=(), writes=[("GT",)], dma=True)
        A3 = self.A[:, :].rearrange("p (k t) -> p k t", k=NC)
        tiles = [t for t in range(c.NT) if not (final_seq is not None and t == 0)]

        def views(idx):
            t = tiles[idx]
            ts, n = c.tile_rng(t)
            hs, s = idx % 3, idx % 2
            return t, ts, n, hs, s, HT[hs][0:n, :], HN[s][0:n, :], SSQ[s][0:n, :], RSTD[s][0:n, :], GT[0:n, :]

        def st_load(idx):
            t, ts, n, hs, s, ht, hn, ssq, rstd, gt = views(idx)
            hsrc = h[ts:ts + n, :]
            self.op("sp", lambda e, ht=ht, hsrc=hsrc: e.dma_start(out=ht, in_=hsrc),
                    reads=[("h", t, q) for q in range(NC)], writes=[("HT", hs)], dma=True)

        def st_1a(idx):
            t, ts, n, hs, s, ht, hn, ssq, rstd, gt = views(idx)
            self.op("dve", lambda e, ssq=ssq: e.memset(ssq, 0.0), reads=(), writes=[("SSQ", s)])
            self.op("act", lambda e, hn=hn, ht=ht, ssq=ssq: e.activation(hn, ht, AF.Square, accum_out=ssq),
                    reads=[("HT", hs), ("SSQ", s)], writes=[("HN", s), ("SSQ", s)])
            self.op("act", lambda e, rstd=rstd, ssq=ssq, epsn=EPS[0:n, :]: e.activation(rstd, ssq, AF.Sqrt, bias=epsn, scale=1.0 / D),
                    reads=[("SSQ", s), ("EPS",)], writes=[("RSTD", s)])

        def st_1b(idx):
            t, ts, n, hs, s, ht, hn, ssq, rstd, gt = views(idx)
            self.op("dve", lambda e, rstd=rstd: e.reciprocal(rstd, rstd),
                    reads=[("RSTD", s)], writes=[("RSTD", s)])

        def st_2(idx):
            t, ts, n, hs, s, ht, hn, ssq, rstd, gt = views(idx)
            if final_seq is None:
                self.op("dve", lambda e, hn=hn, ht=ht, rstd=rstd, gt=gt:
                        e.scalar_tensor_tensor(hn, ht, rstd, gt, ALU.mult, ALU.mult),
                        reads=[("HT", hs), ("RSTD", s), ("GT",), ("HN", s)], writes=[("HN", s)])
                for k0 in range(0, NC, 8):
                    kk = min(8, NC - k0)
                    bank = 6 + (k0 // 8) % 2
                    pb = self.ps[bank][:, :].bitcast(BF16)
                    ident = self.ident[0:n, 0:n]

                    def fn(e, k0=k0, kk=kk, pb=pb, hn=hn, ident=ident, n=n):
                        for q in range(kk):
                            ins = e.transpose(pb[:, q * 128:q * 128 + n], hn[:, (k0 + q) * 128:(k0 + q + 1) * 128], ident)
                        return ins
                    self.op("pe", fn, reads=[("HN", s), ("CB",)], writes=[("ps", bank)])
                    src = pb[:, 0:kk * 128].rearrange("p (k t) -> p k t", k=kk)[:, :, 0:n]
                    dst = A3[:, k0:k0 + kk, ts:ts + n]
                    self.copy_op(self.evac_engine(), dst, src, reads=[("ps", bank)],
                                 writes=[("A", k, t) for k in range(k0, k0 + kk)])
            else:
                self.op("dve", lambda e, ht=ht, rstd=rstd, gt=gt:
                        e.scalar_tensor_tensor(ht, ht, rstd, gt, ALU.mult, ALU.mult),
                        reads=[("HT", hs), ("RSTD", s), ("GT",)], writes=[("HT", hs)])
                dst = self.dt["out"][final_seq, ts - NMETA:ts - NMETA + n, :]
                self.op("sp", lambda e, dst=dst, ht=ht: e.dma_start(out=dst, in_=ht),
                        reads=[("HT", hs)], writes=[("out", final_seq, t)], dma=True)

        nt = len(tiles)
        for i in range(min(2, nt)):
            st_load(i)
        st_1a(0)
        st_1b(0)
        for i in range(nt):
            if i + 2 < nt:
                st_load(i + 2)
            if i + 1 < nt:
                st_1a(i + 1)
            st_2(i)
            if i + 1 < nt:
                st_1b(i + 1)
        rec.barrier()

    def bt_col(self, h, delta):
        key = (h, delta)
        if key not in self.bt_cols:
            assert len(self.bt_cols) < self.NBT
            self.bt_cols[key] = len(self.bt_cols)
        return self.bt_cols[key]

    def act_width(self, h):
        c = self.cfg
        if c.force_w is not None:
            return c.force_w
        sl = alibi_slope(h, c.NC)
        for w in (512, 256, 128):
            if sl * w / 2 <= 50.0:
                return w
        raise AssertionError

    def flush_deferred(self):
        d = self.deferred
        if d is not None:
            self.deferred = None
            d()

    def da_head(self, layer_idx, hd, j, on_slot, NLAM, GS, final_cb=None):
        c = self.cfg
        T, WCA = c.T, c.WCA
        QT = self.B[:, self.oQT + j * T:self.oQT + (j + 1) * T]
        KT = self.B[:, self.oKT + j * T:self.oKT + (j + 1) * T]
        Vb = self.B[:, self.oV:self.oV + c.NT * WCA]
        ONb = self.ON[on_slot]
        o = self.oSP
        TMP = [self.Bf32(o + 1024 * i, 512) for i in range(4)]
        SQ = self.B[:, o + 4096:o + 4096 + 512]
        wact = self.act_width(hd)
        order = list(range(1, c.NQ)) + [0]
        for ci in order:
            q0, N = c.chunk_rng(ci)
            tw = min(128, N)
            w = min(wact, N)
            ktiles = [0] if ci == 0 else [0] + list(range(1, 4 * ci + 1))
            diag0 = 0 if ci == 0 else 4 * (ci - 1) + 1
            nk = len(ktiles)

            def geom(t):
                kcol, kn = c.tile_rng(t)
                isdiag = (t >= diag0) if ci > 0 else True
                lo = tw * (t - diag0) if isdiag else 0
                return kcol, kn, isdiag, lo

            def stage_qk(ki):
                t = ktiles[ki]
                kcol, kn, isdiag, lo = geom(t)
                b = self.da_cnt % 2
                self.da_cnt += 1
                bsel[ki] = b
                for m in range(2):
                    bank = 2 * b + m
                    out = self.ps[bank][0:kn, lo:N]
                    lhsT = KT[64 * m:64 * m + 64, kcol:kcol + kn]
                    rhs = QT[64 * m:64 * m + 64, q0 + lo:q0 + N]
                    self.op("pe", lambda e, out=out, lhsT=lhsT, rhs=rhs: e.matmul(out, lhsT, rhs, start=True, stop=True),
                            reads=[("QT", j, ci), ("KT", j, c.chunk_of_tile(t))], writes=[("ps", bank)])
                for m in range(2):
                    bank = 2 * b + m
                    Pt = self.P[m][b]
                    for blk in range(N // w):
                        r0, r1 = max(lo, blk * w), (blk + 1) * w
                        if r1 <= r0:
                            continue
                        ref = q0 + blk * w + w // 2
                        col = self.bt_col(hd, kcol - ref)
                        bias = self.BT[0:kn, col:col + 1]
                        out = Pt[0:kn, r0:r1]
                        in_ = self.ps[bank][0:kn, r0:r1]
                        self.op("act", lambda e, out=out, in_=in_, bias=bias:
                                e.activation(out, in_, AF.Exp, bias=bias, scale=0.125),
                                reads=[("ps", bank), ("BT",)], writes=[("P", m, b, blk)])
                    if isdiag:
                        blk_ap = Pt[0:kn, lo:lo + tw]
                        mask = self.tri_le[0:kn, 0:tw]
                        self.op("dve", lambda e, blk_ap=blk_ap, mask=mask: e.tensor_tensor(blk_ap, blk_ap, mask, ALU.mult),
                                reads=[("P", m, b, lo // w), ("CB",)], writes=[("P", m, b, lo // w)])

            def stage_av(ki):
                t = ktiles[ki]
                kcol, kn, isdiag, lo = geom(t)
                b = bsel[ki]
                for m in range(2):
                    Pt = self.P[m][b]
                    rhs = Pt[0:kn, lo:N]
                    vl = Vb[0:kn, t * WCA + j * 128:t * WCA + (j + 1) * 128]
                    on = self.ones[0:kn, 0:128]
                    ob, zb = 4 + m, 6 + m
                    oo = self.ps[ob][:, lo:N]
                    zo = self.ps[zb][:, lo:N]
                    st, sp_ = (ki == 0), (ki == nk - 1)

                    def fn(e, oo=oo, zo=zo, vl=vl, on=on, rhs=rhs, st=st, sp_=sp_):
                        e.matmul(oo, vl, rhs, start=st, stop=sp_)
                        return e.matmul(zo, on, rhs, start=st, stop=sp_)
                    self.op("pe", fn, reads=[("P", m, b, q_) for q_ in range(N // w)] + [("V", t), ("CB",)],
                            writes=[("ps", ob), ("ps", zb)])

            bsel = {}
            for step in range(nk + 1):
                if step < nk:
                    stage_qk(step)
                if step >= 1:
                    stage_av(step - 1)
                if step == 2:
                    self.flush_deferred()
            self.flush_deferred()
            RZ0, RZ1, A0, B0 = [x[:, 0:N] for x in TMP]
            O0, O1, Z0, Z1 = [self.ps[i][:, 0:N] for i in (4, 5, 6, 7)]
            self.op("dve", lambda e, RZ0=RZ0, Z0=Z0: e.reciprocal(RZ0, Z0), reads=[("ps", 6)], writes=[("TMP", 0)])
            self.op("dve", lambda e, A0=A0, O0=O0, RZ0=RZ0: e.tensor_tensor(A0, O0, RZ0, ALU.mult),
                    reads=[("ps", 4), ("TMP", 0)], writes=[("TMP", 2)])
            self.op("dve", lambda e, RZ1=RZ1, Z1=Z1: e.reciprocal(RZ1, Z1), reads=[("ps", 7)], writes=[("TMP", 1)])
            self.op("dve", lambda e, B0=B0, O1=O1, RZ1=RZ1: e.tensor_tensor(B0, O1, RZ1, ALU.mult),
                    reads=[("ps", 5), ("TMP", 1)], writes=[("TMP", 3)])
            self.op("dve", lambda e, A0=A0, B0=B0: e.scalar_tensor_tensor(A0, B0, NLAM, A0, ALU.mult, ALU.add),
                    reads=[("TMP", 2), ("TMP", 3), ("LAM",)], writes=[("TMP", 2)])
            self.flush_deferred()
            sq = SQ[:, 0:N]
            ond = ONb[:, q0:q0 + N]
            is_last = (ci == order[-1])

            def tail(sq=sq, A0=A0, RZ0=RZ0, N=N, ond=ond, is_last=is_last):
                self.op("act", lambda e: e.activation(sq, A0, AF.Square),
                        reads=[("TMP", 2)], writes=[("SQ",)])
                ssb = self.ps[0][:, 0:N]
                on = self.ones[:, 0:128]
                self.op("pe", lambda e: e.matmul(ssb, on, sq, start=True, stop=True),
                        reads=[("SQ",), ("CB",)], writes=[("ps", 0)])
                rs = RZ0
                self.op("act", lambda e: e.activation(rs, ssb, AF.Sqrt, bias=self.EPS, scale=1.0 / 128),
                        reads=[("ps", 0), ("EPS",)], writes=[("TMP", 0)])
                self.op("dve", lambda e: e.reciprocal(rs, rs),
                        reads=[("TMP", 0)], writes=[("TMP", 0)])
                self.op("dve", lambda e: e.scalar_tensor_tensor(ond, A0, GS, rs, ALU.mult, ALU.mult),
                        reads=[("TMP", 2), ("TMP", 0), ("LAM",)], writes=[("ON", on_slot)])
                if is_last and final_cb is not None:
                    final_cb()
            self.deferred = tail

    def sb_head(self, hd, j, on_slot):
        c = self.cfg
        T, WCA = c.T, c.WCA
        QT = self.B[:, self.oQT + j * T:self.oQT + (j + 1) * T]
        KT = self.B[:, self.oKT + j * T:self.oKT + (j + 1) * T]
        Vb = self.B[:, self.oV:self.oV + c.NT * WCA]
        ONb = self.ON[on_slot]
        o = self.oSP
        E1 = [self.Bf32(o + 1024 * i, 512) for i in range(2)]
        SPb = [self.B[:, o + 2048 + 512 * i:o + 2048 + 512 * (i + 1)] for i in range(2)]
        SPACC = [self.B[:, o + 3072 + 512 * i:o + 3072 + 512 * (i + 1)] for i in range(2)]
        P4 = [self.P[0][0], self.P[0][1], self.P[1][0], self.P[1][1]]
        scale = 128.0 ** -0.5
        G = []
        for ci in range(c.NQ):
            q0, N = c.chunk_rng(ci)
            tw = min(128, N)
            if ci == 0:
                ktiles, diag0 = [0], 0
            else:
                diag0 = 4 * (ci - 1) + 1
                ktiles = list(range(4 * ci, 0, -1)) + [0]
            nk = len(ktiles)
            for ki, t in enumerate(ktiles):
                kcol, kn = c.tile_rng(t)
                isdiag = (t >= diag0) if ci > 0 else True
                lo = tw * (t - diag0) if isdiag else 0
                G.append(dict(ci=ci, q0=q0, N=N, tw=tw, ki=ki, nk=nk, t=t, kcol=kcol, kn=kn,
                              isdiag=isdiag, lo=lo, cb=ci % 2, first=(ki == 0), last=(ki == nk - 1)))
        for g, d in enumerate(G):
            d["b"] = g % 2
            d["zb"] = g % 4

        def st_A(d):
            N, lo, kn, zb, cb = d["N"], d["lo"], d["kn"], d["zb"], d["cb"]
            if d["first"]:
                acc_n = SPACC[cb][:, 0:N]
                self.op("dve", lambda e, acc_n=acc_n: e.memset(acc_n, 0.0), reads=(), writes=[("SPACC", cb)])
                Ops = self.ps[6 + cb][:, 0:N]
                zl = self.zeros[:, 0:128]
                zr = QT[:, d["q0"]:d["q0"] + N]
                self.op("pe", lambda e, Ops=Ops, zl=zl, zr=zr: e.matmul(Ops, zl, zr, start=True, stop=False),
                        reads=[("QT", j, d["ci"]), ("CB",)], writes=[("ps", 6 + cb)])
            z = self.ps[zb][0:kn, lo:N]
            lhsT = KT[:, d["kcol"]:d["kcol"] + kn]
            rhs = QT[:, d["q0"] + lo:d["q0"] + N]
            self.op("pe", lambda e, z=z, lhsT=lhsT, rhs=rhs: e.matmul(z, lhsT, rhs, start=True, stop=True),
                    reads=[("QT", j, d["ci"]), ("KT", j, c.chunk_of_tile(d["t"]))], writes=[("ps", zb)])

        def st_B(d):
            N, lo, kn, zb, b, tw = d["N"], d["lo"], d["kn"], d["zb"], d["b"], d["tw"]
            z = self.ps[zb][0:kn, lo:N]
            e1 = E1[b][0:kn, lo:N]
            self.op("act", lambda e, e1=e1, z=z: e.activation(e1, z, AF.Exp, scale=scale),
                    reads=[("ps", zb)], writes=[("E1", b)])
            sp = SPb[b][0:kn, lo:N]
            self.op("act", lambda e, sp=sp, e1=e1: e.activation(sp, e1, AF.Ln, bias=1.0),
                    reads=[("E1", b)], writes=[("SP", b)])
            if d["isdiag"]:
                blk_ap = SPb[b][0:kn, lo:lo + tw]
                mask = self.tri_lt[0:kn, 0:tw]
                self.op("dve", lambda e, blk_ap=blk_ap, mask=mask: e.tensor_tensor(blk_ap, blk_ap, mask, ALU.mult),
                        reads=[("SP", b), ("CB",)], writes=[("SP", b)])

        def st_D(d):
            N, lo, kn, zb, b, cb = d["N"], d["lo"], d["kn"], d["zb"], d["b"], d["cb"]
            z = self.ps[zb][0:kn, lo:N]
            sp = SPb[b][0:kn, lo:N]
            nti = self.ntincl[0:kn, 0:kn]
            non = self.nones[:, 0:kn]
            acc = SPACC[cb][:, lo:N]
            first = d["first"]

            def fn(e, z=z, nti=nti, sp=sp, non=non, acc=acc, first=first):
                ins = e.matmul(z, nti, sp, start=False, stop=first, skip_group_check=True)
                if not first:
                    ins = e.matmul(z, non, acc, start=False, stop=True, skip_group_check=True)
                return ins
            self.op("pe", fn, reads=[("SP", b), ("CB",)] + ([] if first else [("SPACC", cb)]), writes=[("ps", zb)])
            if not d["last"]:
                accw = SPACC[cb][0:kn, lo:N]
                self.op("dve", lambda e, accw=accw, sp=sp: e.tensor_tensor(accw, accw, sp, ALU.add),
                        reads=[("SP", b), ("SPACC", cb)], writes=[("SPACC", cb)])

        def st_E(d):
            N, lo, kn, zb, tw = d["N"], d["lo"], d["kn"], d["zb"], d["tw"]
            z = self.ps[zb][0:kn, lo:N]
            p = P4[zb][0:kn, lo:N]
            self.op("act", lambda e, p=p, z=z: e.activation(p, z, AF.Exp, scale=scale),
                    reads=[("ps", zb)], writes=[("PS", zb)])
            if d["isdiag"]:
                blk_ap = P4[zb][0:kn, lo:lo + tw]
                mask = self.tri_lt[0:kn, 0:tw]
                self.op("dve", lambda e, blk_ap=blk_ap, mask=mask: e.tensor_tensor(blk_ap, blk_ap, mask, ALU.mult),
                        reads=[("PS", zb), ("CB",)], writes=[("PS", zb)])

        def st_F(d):
            N, lo, kn, zb, cb, t = d["N"], d["lo"], d["kn"], d["zb"], d["cb"], d["t"]
            p = P4[zb][0:kn, lo:N]
            vl = Vb[0:kn, t * WCA + j * 128:t * WCA + (j + 1) * 128]
            oo = self.ps[6 + cb][:, lo:N]
            self.op("pe", lambda e, oo=oo, vl=vl, p=p, last=d["last"]: e.matmul(oo, vl, p, start=False, stop=last),
                    reads=[("PS", zb), ("V", t)], writes=[("ps", 6 + cb)])
            if d["last"]:
                ond = ONb[:, d["q0"]:d["q0"] + N]
                Ops = self.ps[6 + cb][:, 0:N]
                self.copy_op("dve", ond, Ops, reads=[("ps", 6 + cb)], writes=[("ON", on_slot)])

        ng = len(G)
        for step in range(ng + 3):
            if step < ng:
                st_A(G[step])
            if 1 <= step <= ng:
                st_D(G[step - 1])
            if step < ng:
                st_B(G[step])
            if 2 <= step <= ng + 1:
                st_E(G[step - 2])
            if 3 <= step:
                st_F(G[step - 3])

    def store_on(self, hd, on_slot):
        dst = self.dt["ons"][hd]
        src = self.ON[on_slot][:, :]
        self.op("sp", lambda e: e.dma_start(out=dst, in_=src), reads=[("ON", on_slot)], writes=[("ons", hd)], dma=True)

    def build_seq(self, seq):
        c = self.cfg
        D, T, NC, NT, G, WCA, WC = c.D, c.T, c.NC, c.NT, c.G, c.WCA, c.WC
        dt = self.dt
        rec = self.rec
        h = dt["h"]
        self.op("sp", lambda e: e.dma_start(out=h[0:NMETA, :], in_=dt["meta"]),
                reads=(), writes=[("h", 0, q) for q in range(NC)], dma=True)
        for t in range(1, NT):
            ts, n = c.tile_rng(t)
            src = dt["x"][seq, ts - NMETA:ts - NMETA + n, :]
            dst = h[ts:ts + n, :]
            self.op("sp", lambda e, dst=dst, src=src: e.dma_start(out=dst, in_=src),
                    reads=(), writes=[("h", t, q) for q in range(NC)], dma=True)

        A = self.A
        QTb = self.B[:, self.oQT:self.oQT + G * T]
        KTb = self.B[:, self.oKT:self.oKT + G * T]
        Vb = self.B[:, self.oV:self.oV + NT * WCA]
        jobs = []
        jtags = []
        self.jtag = "init"

        class _JL(list):
            def append(jl, x):
                list.append(jl, x)
                jtags.append(self.jtag)
        jobs = _JL()

        def akeys_chunk(ci):
            return self.a_keys_chunk(ci)

        def akeys_tile(t):
            return [("A", k, t) for k in range(NC)]

        def evac_to(buf, keyname):
            def ev(j, ci, ps_ap, bank):
                cs, n = c.chunk_rng(ci)
                dst = buf[:, j * T + cs:j * T + cs + n]
                self.copy_op(self.evac_engine(), dst, ps_ap, reads=[("ps", bank)], writes=[(keyname, j, ci)])
            return ev

        def evac_v(t, ps_ap, bank):
            ts, n = c.tile_rng(t)
            dst = Vb[0:n, t * WCA:(t + 1) * WCA]
            self.copy_op(self.evac_engine(), dst, ps_ap, reads=[("ps", bank)], writes=[("V", t)])

        def job_gF(src, ncols, evac):
            jobs.append((src, ncols, lambda wv, wk: self.gemm_F(A, akeys_chunk, wv, wk, ncols, evac)))

        def job_gT(src, ncols, evac, pre=None):
            jobs.append((src, ncols, lambda wv, wk: self.gemm_T(A, akeys_tile, wv, wk, ncols, evac, pre)))

        def job(fn):
            jobs.append((None, 0, lambda wv, wk: fn()))

        def load_on_to_A(heads=None):
            for k in (range(NC) if heads is None else heads):
                dst = A[:, k * T:(k + 1) * T]
                src = dt["ons"][k]
                self.op("sp", lambda e, dst=dst, src=src: e.dma_start(out=dst, in_=src),
                        reads=[("ons", k)], writes=[("A", k, t) for t in range(NT)], dma=True)

        def o_proj(w_ap):
            self.jtag = "oproj"
            job(lambda: load_on_to_A(range((c.NG - 1) * G, NC)))
            for cbk in range(D // WC):
                pre, ev = self.make_rmw(cbk * WC, WC)
                job_gT(w_ap[:, cbk * WC:(cbk + 1) * WC], WC, ev, pre)

        def mlp(layer):
            self.jtag = "norm"
            job(lambda: self.norm(dt["mlp_g"][layer]))
            H1 = self.B
            for qtr in range(4):
                for ft in range(D // WC):
                    f0 = qtr * D + ft * WC

                    def ev_up(j, ci, ps_ap, bank, ft=ft):
                        cs, n = c.chunk_rng(ci)
                        fc = ft * (WC // 128) + j
                        dst = H1[:, fc * T + cs:fc * T + cs + n]
                        ri = self.rt_i = (getattr(self, "rt_i", 0) + 1) % 2
                        rt = self.RT[ri][:, 0:n]
                        self.op("act", lambda e, rt=rt, ps_ap=ps_ap: e.activation(rt, ps_ap, AF.Relu),
                                reads=[("ps", bank)], writes=[("RT", ri)])
                        self.op("dve", lambda e, dst=dst, rt=rt: e.tensor_tensor(dst, rt, rt, ALU.mult),
                                reads=[("RT", ri)], writes=[("H1", fc, ci)])
                    self.jtag = "mlp_up"
                    job_gF(dt["wup"][layer][:, f0:f0 + WC], WC, ev_up)
                for cbk in range(D // WC):
                    pre, ev = self.make_rmw(cbk * WC, WC)

                    def gT(wv, wk, ev=ev, pre=pre):
                        self.gemm_T(H1, lambda t: [("H1", fc, c.chunk_of_tile(t)) for fc in range(NC)], wv, wk, WC, ev, pre)
                    self.jtag = "mlp_down"
                    jobs.append((dt["wdown"][layer][qtr * D:(qtr + 1) * D, cbk * WC:(cbk + 1) * WC], WC, gT))

        for l in range(c.NA):
            lam_init = 0.8 - 0.6 * math.exp(-0.3 * l)
            self.jtag = "norm"
            job(lambda l=l: self.norm(dt["attn_g"][l]))
            NLAM = self.SM[:, 260:261]
            GS = self.SM[:, 261:262]

            def lam_setup(l=l, lam_init=lam_init):
                SM = self.SM
                for i, nm in enumerate(("lq1", "lk1", "lq2", "lk2")):
                    dst = SM[:, 64 * i:64 * (i + 1)]
                    src = dt[nm][l].partition_broadcast(128)
                    self.op("sp", lambda e, dst=dst, src=src: e.dma_start(out=dst, in_=src), reads=(), writes=[("LQ", i)], dma=True)
                gs_raw = SM[:, 262:263]
                src = dt["subln"][l:l + 1, :].rearrange("a p -> p a")
                self.op("sp", lambda e: e.dma_start(out=gs_raw, in_=src), reads=(), writes=[("GSR",)], dma=True)
                pr = [SM[:, 264 + 2 * i:265 + 2 * i] for i in range(2)]
                for i in range(2):
                    a = SM[:, 128 * i:128 * i + 64]
                    b_ = SM[:, 128 * i + 64:128 * i + 128]
                    self.op("dve", lambda e, a=a, b_=b_: e.tensor_tensor(a, a, b_, ALU.mult),
                            reads=[("LQ", 2 * i), ("LQ", 2 * i + 1)], writes=[("LQ", 2 * i)])
                    self.op("dve", lambda e, a=a, p_=pr[i]: e.tensor_reduce(p_, a, AX.X, ALU.add),
                            reads=[("LQ", 2 * i)], writes=[("PR", i)])
                    self.op("act", lambda e, p_=pr[i]: e.activation(p_, p_, AF.Exp), reads=[("PR", i)], writes=[("PR", i)])
                self.op("dve", lambda e: e.tensor_tensor(NLAM, pr[1], pr[0], ALU.subtract),
                        reads=[("PR", 0), ("PR", 1)], writes=[("LAM",)])
                self.op("dve", lambda e: e.tensor_scalar(NLAM, NLAM, -lam_init, None, ALU.add),
                        reads=[("LAM",)], writes=[("LAM",)])
                self.op("dve", lambda e: e.tensor_scalar(GS, gs_raw, 1.0 - lam_init, None, ALU.mult),
                        reads=[("GSR",), ("LAM",)], writes=[("LAM",)])
            job(lam_setup)
            for g in range(c.NG):
                wq = dt["wqkv"][l][:, g * WCA:(g + 1) * WCA]
                wk_ = dt["wqkv"][l][:, D + g * WCA:D + (g + 1) * WCA]
                wv_ = dt["wqkv"][l][:, 2 * D + g * WCA:2 * D + (g + 1) * WCA]
                self.jtag = "qkv"
                job_gF(wq, WCA, evac_to(QTb, "QT"))
                job_gF(wk_, WCA, evac_to(KTb, "KT"))
                job_gT(wv_, WCA, evac_v)
                if g == c.NG - 1 and g > 0:
                    self.jtag = "qkv"
                    job(lambda: load_on_to_A(range(0, (c.NG - 1) * G)))
                for j in range(G):
                    hd = g * G + j

                    def att(l=l, hd=hd, j=j, NLAM=NLAM, GS=GS):
                        slot = hd % 2
                        self.da_head(l, hd, j, slot, NLAM, GS, final_cb=lambda: self.store_on(hd, slot))
                    self.jtag = "da_att"
                    job(att)
            o_proj(dt["wo_a"][l])
            mlp(l)

        if c.NB > 0:
            self.jtag = "norm"
            job(lambda: self.norm(dt["kvg"][0]))
            self.jtag = "kv"
            for g in range(c.NG):
                def ev_k(j, ci, ps_ap, bank):
                    cs, n = c.chunk_rng(ci)
                    dst = KTb[:, j * T + cs:j * T + cs + n]
                    self.copy_op(self.evac_engine(), dst, ps_ap, reads=[("ps", bank)], writes=[("KT", j, ci)])
                job_gF(dt["wk"][:, g * WCA:(g + 1) * WCA], WCA, ev_k)

                def st_k(g=g):
                    dst = dt["ks"][g * G:(g + 1) * G].rearrange("h p t -> p h t")
                    src = KTb.rearrange("p (h t) -> p h t", h=G)
                    self.op("sp", lambda e: e.dma_start(out=dst, in_=src),
                            reads=[("KT", j, ci) for j in range(G) for ci in range(c.NQ)], writes=[("ks", g)], dma=True)
                job(st_k)
                job_gT(dt["wv"][:, g * WCA:(g + 1) * WCA], WCA, evac_v)

                def st_v(g=g):
                    vs = dt["vs"]
                    self.op("sp", lambda e: e.dma_start(out=vs[0:NMETA, g * WCA:(g + 1) * WCA], in_=Vb[0:NMETA, 0:WCA]),
                            reads=[("V", 0)], writes=[("vs", g, 0)], dma=True)
                    dst = vs[NMETA:, g * WCA:(g + 1) * WCA].rearrange("(t p) c -> p t c", p=128)
                    src = Vb[:, WCA:].rearrange("p (t c) -> p t c", c=WCA)
                    self.op("sp", lambda e: e.dma_start(out=dst, in_=src),
                            reads=[("V", t) for t in range(1, NT)], writes=[("vs", g, 1)], dma=True)
                job(st_v)

        for jb in range(c.NB):
            l = c.NA + jb
            self.jtag = "norm"
            job(lambda l=l: self.norm(dt["attn_g"][l]))
            for g in range(c.NG):
                self.jtag = "q_sb"
                job_gF(dt["wq_b"][jb][:, g * WCA:(g + 1) * WCA], WCA, evac_to(QTb, "QT"))

                def ld_kv(g=g):
                    src = dt["ks"][g * G:(g + 1) * G].rearrange("h p t -> p h t")
                    dst = KTb.rearrange("p (h t) -> p h t", h=G)
                    self.op("sp", lambda e: e.dma_start(out=dst, in_=src),
                            reads=[("ks", g)], writes=[("KT", j, ci) for j in range(G) for ci in range(c.NQ)], dma=True)
                    vs = dt["vs"]
                    self.op("sp", lambda e: e.dma_start(out=Vb[0:NMETA, 0:WCA], in_=vs[0:NMETA, g * WCA:(g + 1) * WCA]),
                            reads=[("vs", g, 0)], writes=[("V", 0)], dma=True)
                    src2 = vs[NMETA:, g * WCA:(g + 1) * WCA].rearrange("(t p) c -> p t c", p=128)
                    dst2 = Vb[:, WCA:].rearrange("p (t c) -> p t c", c=WCA)
                    self.op("sp", lambda e: e.dma_start(out=dst2, in_=src2),
                            reads=[("vs", g, 1)], writes=[("V", t) for t in range(1, NT)], dma=True)
                job(ld_kv)
                if g == c.NG - 1 and g > 0:
                    job(lambda: load_on_to_A(range(0, (c.NG - 1) * G)))
                for j in range(G):
                    hd = g * G + j

                    def att(hd=hd, j=j):
                        slot = hd % 2
                        self.sb_head(hd, j, slot)
                        self.store_on(hd, slot)
                    self.jtag = "sb_att"
                    job(att)
            o_proj(dt["wo_b"][jb])
            mlp(l)

        self.jtag = "norm"
        job(lambda: self.norm(dt["fng"][0], final_seq=seq))

        widx = [i for i, jb_ in enumerate(jobs) if jb_[0] is not None]
        loaded = {}
        nxt = [0]

        owner = self.hs_owner
        done = self.w_done

        def ensure(upto_pos):
            while nxt[0] < len(widx) and nxt[0] <= upto_pos:
                i = widx[nxt[0]]
                src, ncols, _ = jobs[i]
                hs, nh, _p = self.peek_w(ncols > 256)
                if any(owner[hs + q] is not None and owner[hs + q] not in done for q in range(nh)):
                    break
                for q in range(nh):
                    owner[hs + q] = (seq, i)
                loaded[i] = self.load_w(src, ncols)
                nxt[0] += 1

        wpos = {i: p for p, i in enumerate(widx)}
        cur = 0
        for i, (src, ncols, fn) in enumerate(jobs):
            self.rec.tag = (seq, jtags[i])
            if jtags[i] != "da_att":
                self.flush_deferred()
            if src is not None:
                p = wpos[i]
                depth = 1 if ncols > 256 else 2
                ensure(p + depth)
                cur = p + 1
                wv, wk = loaded.pop(i)
                fn(wv, wk)
                done.add((seq, i))
            else:
                ensure(cur)
                fn(None, None)

    def bt_table(self):
        c = self.cfg
        bt = np.zeros((128, self.NBT), np.float32)
        p = np.arange(128, dtype=np.float64)
        for (hd, delta), col in self.bt_cols.items():
            bt[:, col] = (alibi_slope(hd, c.NC) * (p + delta)).astype(np.float32)
        return bt


def const_cb16():
    p = np.arange(128)[:, None]
    j = np.arange(128)[None, :]
    ident = (p == j)
    ones = np.ones((128, 128), bool)
    tri_le = (p <= j)
    tri_lt = (p < j)
    tincl = (p >= j)
    cb = np.concatenate([ident, ones, tri_le, tri_lt, tincl, np.zeros((128, 128), bool)], axis=1).astype(np.float32)
    neg = -float(np.sqrt(128.0))
    cb = np.concatenate([cb, neg * tincl.astype(np.float32), neg * np.ones((128, 128), np.float32)], axis=1)
    return cb.astype(ml_dtypes.bfloat16)


_CACHE = {}


def build_program(cfg, debug=False):
    nc = bass.Bass("TRN2", target_bir_lowering=False)
    b = Builder(cfg, debug)
    b.rec.max_ops = getattr(cfg, "max_ops", None)
    with ExitStack() as es:
        b.declare(nc)
        b.allocate(nc, es)
        b.op("sp", lambda e: e.dma_start(out=b.CB[:, :], in_=b.dt["cb16"]), reads=(), writes=[("CB",)], dma=True)
        b.op("sp", lambda e: e.dma_start(out=b.BT[:, :], in_=b.dt["bt"]), reads=(), writes=[("BT",)], dma=True)
        b.op("dve", lambda e: e.memset(b.EPS, RMS_EPS), reads=(), writes=[("EPS",)])
        for seq in range(cfg.NSEQ):
            b.build_seq(seq)
        b.rec.max_ops = None
        b.rec.barrier()
        b.op("sp", None, reads=(), writes=())
        print("n_ops", len(b.rec.ops), flush=True)
        sems = {}
        dsems = {}
        for e in ENGINES:
            sems[e] = [es.enter_context(nc.semaphore(f"s_{e}{i}")) for i in range(SEM_RING)]
        for q, K in DMA_RING.items():
            dsems[q] = [es.enter_context(nc.semaphore(f"d_{q}{i}")) for i in range(K)]
        with nc.Block() as block:
            b.rec.emit(nc, block, sems, dsems)
    return nc, b


def run_cfg(cfg, inputs, n_cores=8, trace=False):
    nc, b = build_program(cfg)
    f32 = lambda a: np.ascontiguousarray(np.asarray(a, dtype=np.float32))
    shared = {
        "meta": f32(inputs["meta_tokens"]),
        "attn_g": f32(inputs["attn_norm_g"]),
        "mlp_g": f32(inputs["mlp_norm_g"]),
        "wqkv": f32(inputs["da_w_qkv"]),
        "wo_a": f32(inputs["da_w_o"]),
        "lq1": f32(inputs["da_lambda_q1"]),
        "lk1": f32(inputs["da_lambda_k1"]),
        "lq2": f32(inputs["da_lambda_q2"]),
        "lk2": f32(inputs["da_lambda_k2"]),
        "subln": f32(inputs["da_subln_g"]),
        "kvg": f32(inputs["kv_norm_g"]).reshape(1, -1),
        "wk": f32(inputs["sb_w_k"]),
        "wv": f32(inputs["sb_w_v"]),
        "wq_b": f32(inputs["sb_w_q"]),
        "wo_b": f32(inputs["sb_w_o"]),
        "wup": f32(inputs["mlp_w_up"]),
        "wdown": f32(inputs["mlp_w_down"]),
        "fng": f32(inputs["final_norm_g"]).reshape(1, -1),
        "cb16": const_cb16(),
        "bt": b.bt_table(),
    }
    x = f32(inputs["x"])
    in_maps = []
    for core in range(n_cores):
        m = dict(shared)
        m["x"] = np.ascontiguousarray(x[core * cfg.NSEQ:(core + 1) * cfg.NSEQ])
        in_maps.append(m)
    res = run_bass_kernel_spmd(nc, in_maps, core_ids=list(range(n_cores)), trace=trace)
    out = np.concatenate([np.asarray(r["out"]) for r in res.results], axis=0)
    return out.astype(np.float32), res


def kernel(**inputs):
    cfg = Cfg(D=2048, S=2048, NSEQ=2, NA=2, NB=2)
    out, _ = run_cfg(cfg, inputs, n_cores=8)
    return out
```
